# Optimizing a Trainium2 kernel written in Bass

```python
import math
import jax
import jax.numpy as jnp
from jax import lax
import numpy as np

D_MODEL = 1024
BATCH = 2
SEQ = 8192
DEPTH = 1

D_MIX = D_MODEL
GDN_HEADS = 4
GDN_HEAD_DIM = 128
GDN_WIDTH = GDN_HEADS * GDN_HEAD_DIM
CONV_WIDTH = 5
CHUNK = 64
SWA_HEADS = 8
SWA_HEAD_DIM = 64
SWA_WIDTH = SWA_HEADS * SWA_HEAD_DIM
DILATION_PATTERNS = ((128, 1), (512, 4), (2048, 16))
BAND_BLOCK = 64
REL_BUCKETS = 32
REL_MAX_DISTANCE = 1024
D_FF = 2816
EPS = 1e-6
NEG_BIG = -1e30
SPLITS = (3 * GDN_WIDTH, GDN_WIDTH, 2 * GDN_HEADS, 2 * GDN_HEADS, 3 * SWA_WIDTH)
N_IN = sum(SPLITS)

kernel_name = "hybrid_gdn_dilated_swa_macaron"


def rms_norm(x, w):
    xf = x.astype(jnp.float32)
    y = xf * lax.rsqrt(jnp.mean(xf * xf, axis=-1, keepdims=True) + EPS)
    return (y * w.astype(jnp.float32)).astype(x.dtype)


def l2_normalize(x):
    return x * lax.rsqrt(jnp.sum(x * x, axis=-1, keepdims=True) + EPS)


def swiglu(x, w_gate, w_up, w_down):
    return (jax.nn.silu(x @ w_gate) * (x @ w_up)) @ w_down


def t5_bucket(rel):
    nb = REL_BUCKETS // 2
    bucket = (rel > 0).astype(np.int32) * nb
    n = np.abs(rel)
    max_exact = nb // 2
    large = max_exact + (np.log(np.maximum(n, 1) / max_exact)
                         / math.log(REL_MAX_DISTANCE / max_exact) * (nb - max_exact)).astype(np.int32)
    large = np.minimum(large, nb - 1)
    return (bucket + np.where(n < max_exact, n, large)).astype(np.int32)


def short_conv(x, w):
    c, kw = w.shape
    rhs = jnp.transpose(w).astype(x.dtype)[:, None, :]
    return lax.conv_general_dilated(x, rhs, window_strides=(1,),
                                    padding=((kw // 2, kw // 2),),
                                    dimension_numbers=('NWC', 'WIO', 'NWC'),
                                    feature_group_count=c)


def gated_delta_chunked(q, k, v, g, beta):
    b, h, t, dk = q.shape
    dv = v.shape[-1]
    nc = t // CHUNK
    q = q * (dk ** -0.5)

    def chunks(a):
        return a.reshape(b, h, nc, CHUNK, *a.shape[3:])

    q, k, v, g, beta = chunks(q), chunks(k), chunks(v), chunks(g), chunks(beta)
    g = jnp.cumsum(g, axis=-1)
    incl = jnp.tril(jnp.ones((CHUNK, CHUNK), dtype=bool))
    strict = jnp.tril(jnp.ones((CHUNK, CHUNK), dtype=bool), -1)
    diff = g[..., :, None] - g[..., None, :]
    decay = jnp.where(incl, jnp.exp(jnp.where(incl, diff, 0.0)), 0.0)
    kb = k * beta[..., None]
    lmat = jnp.where(strict, jnp.einsum('bhncd,bhnjd->bhncj', kb, k) * decay, 0.0)
    eye = jnp.eye(CHUNK, dtype=jnp.float32)
    tmat = lax.linalg.triangular_solve(eye + lmat, jnp.broadcast_to(eye, lmat.shape),
                                       left_side=True, lower=True, unit_diagonal=True)
    u = tmat @ (v * beta[..., None])
    w = tmat @ (kb * jnp.exp(g)[..., None])
    intra = jnp.einsum('bhncd,bhnjd->bhncj', q, k) * decay
    qg = q * jnp.exp(g)[..., None]
    g_last = g[..., -1]
    kdec = k * jnp.exp(g_last[..., None] - g)[..., None]

    def step(state, inp):
        u_c, w_c, qg_c, intra_c, kdec_c, gl_c = inp
        v_new = u_c - w_c @ state
        o_c = qg_c @ state + intra_c @ v_new
        state = state * jnp.exp(gl_c)[..., None, None] + jnp.swapaxes(kdec_c, -1, -2) @ v_new
        return state, o_c

    xs = tuple(jnp.moveaxis(a, 2, 0) for a in (u, w, qg, intra, kdec, g_last))
    state0 = jnp.zeros((b, h, dk, dv), jnp.float32)
    _, o = lax.scan(step, state0, xs)
    return jnp.moveaxis(o, 0, 2).reshape(b, h, t, dv)


def _reverse(a):
    return jnp.flip(a, axis=2)


def gdn_mixer(qkv, z, a, beta_logit, conv_w, a_log, dt_bias, norm_w):
    b, s, _ = qkv.shape
    f32 = jnp.float32
    qkv = jax.nn.silu(short_conv(qkv, conv_w)).astype(f32)
    qkv = qkv.reshape(b, s, 3, GDN_HEADS, GDN_HEAD_DIM).transpose(2, 0, 3, 1, 4)
    q, k, v = l2_normalize(qkv[0]), l2_normalize(qkv[1]), qkv[2]
    a = a.astype(f32).reshape(b, s, 2, GDN_HEADS)
    g = -jnp.exp(a_log.astype(f32)) * jax.nn.softplus(a + dt_bias.astype(f32))
    beta = jax.nn.sigmoid(beta_logit.astype(f32).reshape(b, s, 2, GDN_HEADS))
    g = g.transpose(2, 0, 3, 1)
    beta = beta.transpose(2, 0, 3, 1)
    o_fwd = gated_delta_chunked(q, k, v, g[0], beta[0])
    o_bwd = _reverse(gated_delta_chunked(_reverse(q), _reverse(k), _reverse(v),
                                         _reverse(g[1]), _reverse(beta[1])))
    o = (o_fwd + o_bwd).transpose(0, 2, 1, 3)
    zg = jax.nn.silu(z.astype(f32).reshape(b, s, GDN_HEADS, GDN_HEAD_DIM))
    o = rms_norm(o, norm_w) * zg
    return o.reshape(b, s, GDN_WIDTH).astype(z.dtype)


def dilated_band_attention(q, k, v, rel_bias, window, dilation):
    b, s, h, dh = q.shape
    radius = window // (2 * dilation)
    blk = BAND_BLOCK
    length = s // dilation
    nb = -(-length // blk)
    lp = nb * blk

    def to_blocks(a):
        a = a.reshape(b, length, dilation, h, dh).transpose(0, 2, 3, 1, 4)
        a = jnp.pad(a, ((0, 0), (0, 0), (0, 0), (0, lp - length), (0, 0)))
        return a.reshape(b, dilation, h, nb, blk, dh)

    def band(a):
        ap = jnp.pad(a, ((0, 0), (0, 0), (0, 0), (1, 1), (0, 0), (0, 0)))
        return jnp.concatenate([ap[:, :, :, :-2], ap[:, :, :, 1:-1], ap[:, :, :, 2:]], axis=-2)

    qb = to_blocks(q)
    kw = band(to_blocks(k))
    vw = band(to_blocks(v))
    qi = np.arange(blk)[:, None]
    ki = np.arange(3 * blk)[None, :]
    rel = ki - blk - qi
    key_t = (np.arange(nb)[:, None, None] - 1) * blk + ki[None]
    valid = (np.abs(rel)[None] <= radius) & (key_t >= 0) & (key_t < length)
    bias = jnp.transpose(rel_bias[t5_bucket(rel * dilation)], (2, 0, 1))[:, None]
    logits = jnp.einsum('bdhnqe,bdhnke->bdhnqk', qb, kw, preferred_element_type=jnp.float32)
    logits = jnp.where(valid, logits + bias.astype(jnp.float32), NEG_BIG)
    m = jnp.max(logits, axis=-1, keepdims=True)
    p = jnp.exp(logits - m)
    den = jnp.sum(p, axis=-1, keepdims=True)
    o = jnp.einsum('bdhnqk,bdhnke->bdhnqe', p, vw.astype(jnp.float32)) / den
    lse = (m + jnp.log(den))[..., 0]
    o = o.reshape(b, dilation, h, lp, dh)[:, :, :, :length].transpose(0, 3, 1, 2, 4).reshape(b, s, h, dh)
    lse = lse.reshape(b, dilation, h, lp)[..., :length].transpose(0, 3, 1, 2).reshape(b, s, h)
    return o, lse


def dilated_mixer(qkv, q_norm_w, k_norm_w, rel_bias):
    b, s, _ = qkv.shape
    qkv = qkv.reshape(b, s, 3, SWA_HEADS, SWA_HEAD_DIM)
    q = rms_norm(qkv[:, :, 0], q_norm_w) * (SWA_HEAD_DIM ** -0.5)
    k = rms_norm(qkv[:, :, 1], k_norm_w)
    v = qkv[:, :, 2]
    outs, lses = [], []
    for window, dilation in DILATION_PATTERNS:
        o_p, lse_p = dilated_band_attention(q, k, v, rel_bias, window, dilation)
        outs.append(o_p)
        lses.append(lse_p)
    wts = jax.nn.softmax(jnp.stack(lses, axis=0), axis=0)
    o = jnp.sum(wts[..., None] * jnp.stack(outs, axis=0), axis=0)
    return o.reshape(b, s, SWA_WIDTH).astype(qkv.dtype)


def setup_inputs(seed: int = 0) -> dict:
    key = jax.random.key(seed)
    ks = jax.random.split(key, 24)
    f32 = jnp.float32

    def dense(k, shape, fan_in):
        return jax.random.normal(k, shape, f32) * (fan_in ** -0.5)

    def gain(k, shape):
        return 1.0 + 0.02 * jax.random.normal(k, shape, f32)

    x = jax.random.normal(ks[0], (BATCH, SEQ, D_MODEL), f32)
    ffn1_norm = gain(ks[1], (DEPTH, D_MODEL))
    ffn1_w_gate = dense(ks[2], (DEPTH, D_MODEL, D_FF), D_MODEL)
    ffn1_w_up = dense(ks[3], (DEPTH, D_MODEL, D_FF), D_MODEL)
    ffn1_w_down = dense(ks[4], (DEPTH, D_FF, D_MODEL), D_FF)
    mix_norm = gain(ks[5], (DEPTH, D_MODEL))
    w_in = dense(ks[6], (DEPTH, D_MODEL, N_IN), D_MODEL)
    conv_w = dense(ks[7], (DEPTH, 3 * GDN_WIDTH, CONV_WIDTH), CONV_WIDTH)
    a_log = jnp.log(jax.random.uniform(ks[8], (DEPTH, 2, GDN_HEADS), f32, 1.0, 16.0))
    dt = jnp.exp(jax.random.uniform(ks[9], (DEPTH, 2, GDN_HEADS), f32, math.log(1e-3), math.log(1e-1)))
    dt_bias = dt + jnp.log(-jnp.expm1(-dt))
    gdn_norm_w = gain(ks[10], (DEPTH, GDN_HEAD_DIM))
    q_norm_w = gain(ks[11], (DEPTH, SWA_HEAD_DIM))
    k_norm_w = gain(ks[12], (DEPTH, SWA_HEAD_DIM))
    rel_bias = 0.2 * jax.random.normal(ks[13], (REL_BUCKETS, SWA_HEADS), f32)
    w_out = dense(ks[14], (DEPTH, D_MIX, D_MODEL), D_MIX)
    ffn2_norm = gain(ks[15], (DEPTH, D_MODEL))
    ffn2_w_gate = dense(ks[16], (DEPTH, D_MODEL, D_FF), D_MODEL)
    ffn2_w_up = dense(ks[17], (DEPTH, D_MODEL, D_FF), D_MODEL)
    ffn2_w_down = dense(ks[18], (DEPTH, D_FF, D_MODEL), D_FF)
    final_norm = gain(ks[19], (DEPTH, D_MODEL))
    return {"x": x, "ffn1_norm": ffn1_norm, "ffn1_w_gate": ffn1_w_gate, "ffn1_w_up": ffn1_w_up,
            "ffn1_w_down": ffn1_w_down, "mix_norm": mix_norm, "w_in": w_in, "conv_w": conv_w,
            "a_log": a_log, "dt_bias": dt_bias, "gdn_norm_w": gdn_norm_w, "q_norm_w": q_norm_w,
            "k_norm_w": k_norm_w, "rel_bias": rel_bias, "w_out": w_out, "ffn2_norm": ffn2_norm,
            "ffn2_w_gate": ffn2_w_gate, "ffn2_w_up": ffn2_w_up, "ffn2_w_down": ffn2_w_down,
            "final_norm": final_norm}


def reference(x, ffn1_norm, ffn1_w_gate, ffn1_w_up, ffn1_w_down, mix_norm, w_in, conv_w,
              a_log, dt_bias, gdn_norm_w, q_norm_w, k_norm_w, rel_bias, w_out, ffn2_norm,
              ffn2_w_gate, ffn2_w_up, ffn2_w_down, final_norm):
    split_at = np.cumsum(SPLITS)[:-1].tolist()
    for l in range(DEPTH):
        x = x + 0.5 * swiglu(rms_norm(x, ffn1_norm[l]), ffn1_w_gate[l], ffn1_w_up[l], ffn1_w_down[l])
        h = rms_norm(x, mix_norm[l])
        proj = h @ w_in[l]
        qkv_a, z_a, a_a, b_a, qkv_b = jnp.split(proj, split_at, axis=-1)
        o_a = gdn_mixer(qkv_a, z_a, a_a, b_a, conv_w[l], a_log[l], dt_bias[l], gdn_norm_w[l])
        o_b = dilated_mixer(qkv_b, q_norm_w[l], k_norm_w[l], rel_bias)
        x = x + jnp.concatenate([o_a, o_b], axis=-1) @ w_out[l]
        x = x + 0.5 * swiglu(rms_norm(x, ffn2_norm[l]), ffn2_w_gate[l], ffn2_w_up[l], ffn2_w_down[l])
        x = rms_norm(x, final_norm[l])
    return x
```

```python
import contextlib
import numpy as np
import ml_dtypes
import concourse.bass as bass
import concourse.mybir as mybir
from concourse.bass_utils import run_bass_kernel_spmd

F32 = mybir.dt.float32
BF16 = mybir.dt.bfloat16
AF = mybir.ActivationFunctionType
ALU = mybir.AluOpType

D = 1024
KC = 8
FF = 2816
NF = 22
SEQ = 8192
TOK = 2048
NT = TOK // 128
EPS = 1e-6
PARTS = (6, 6, 5, 5)


class Buf:
    def __init__(self, ap, name="", excl=None):
        self.ap = ap
        self.name = name
        self.excl = self if excl == "self" else excl
        self.w = None
        self.r = {}
        self.dsem = None
        self.dcnt = 0
        self.rsem = None
        self.rcnt = 0

    def __getitem__(self, idx):
        return self.ap[idx]


class Sched:
    ENG = ("pe", "act", "dve", "pool", "sp")

    def __init__(self, nc):
        self.nc = nc
        self.e = {"pe": nc.tensor, "act": nc.scalar, "dve": nc.vector, "pool": nc.gpsimd, "sp": nc.sync}
        self.sem = {k: nc.alloc_semaphore("s_" + k) for k in self.ENG}
        self.cnt = {k: 0 for k in self.ENG}
        self.seen = {k: {} for k in self.ENG}
        self.same_engine_sync = True
        self.nsem = 0
        self.dsems = []
        self.sempool = {}

    def newsem(self, name):
        self.nsem += 1
        return self.nc.alloc_semaphore(f"{name}_{self.nsem}")

    def _wait(self, eng, sem, key, val):
        if val <= 0 or self.seen[eng].get(key, 0) >= val:
            return
        self.e[eng].wait_ge(sem, val)
        self.seen[eng][key] = val

    def _deps(self, eng, reads, writes):
        for b in reads:
            if b.w is not None:
                we, wc = b.w
                if we != eng or (self.same_engine_sync and eng != "pe"):
                    self._wait(eng, self.sem[we], we, wc)
            if b.dsem is not None and b.dcnt:
                self._wait(eng, b.dsem, id(b.dsem), b.dcnt)
        for b in writes:
            if b.w is not None:
                we, wc = b.w
                if we != eng or (self.same_engine_sync and eng != "pe"):
                    self._wait(eng, self.sem[we], we, wc)
            for re_, rc in b.r.items():
                if re_ != eng:
                    self._wait(eng, self.sem[re_], re_, rc)
            if b.dsem is not None and b.dcnt:
                self._wait(eng, b.dsem, id(b.dsem), b.dcnt)
            if b.rsem is not None and b.rcnt:
                self._wait(eng, b.rsem, id(b.rsem), b.rcnt)

    def op(self, eng, fn, reads=(), writes=(), acc=False):
        ex = []
        for b in list(reads) + list(writes):
            if b.excl is not None and b.excl not in ex:
                ex.append(b.excl)
        reads = [b for b in reads if b.excl is None]
        writes = [b for b in writes if b.excl is None] + ex
        self._deps(eng, reads, () if acc else writes)
        ins = fn(self.e[eng])
        self.cnt[eng] += 1
        ins.then_inc(self.sem[eng], 1)
        c = self.cnt[eng]
        for b in reads:
            b.r[eng] = c
        for b in writes:
            b.w = (eng, c)
            if not acc:
                b.r = {}
        return ins

    def dma(self, eng, out_buf, out_ap, in_buf, in_ap, **kw):
        self._deps(eng, [in_buf], [out_buf])
        if out_buf.dsem is None:
            pool = self.sempool.setdefault(eng, [])
            if pool:
                out_buf.dsem, out_buf.dcnt = pool.pop()
            else:
                out_buf.dsem = self.newsem("dw")
            out_buf.dq = eng
            self.dsems.append(out_buf)
        assert getattr(out_buf, "dq", eng) == eng, f"buffer {out_buf.name} written by DMAs from two queues"
        ins = self.e[eng].dma_start(out=out_ap, in_=in_ap, **kw)
        ins.then_inc(out_buf.dsem, 16)
        out_buf.dcnt += 16
        in_buf.rsem = out_buf.dsem
        in_buf.rcnt = out_buf.dcnt
        return ins

    def wait_dma(self, eng, bufs):
        for b in bufs:
            if b.dsem is not None:
                self._wait(eng, b.dsem, id(b.dsem), b.dcnt)

    def end_phase(self):
        self.barrier()
        keep = []
        for b in self.dsems:
            if getattr(b, "persist", False):
                keep.append(b)
            else:
                self.sempool.setdefault(b.dq, []).append((b.dsem, b.dcnt))
        self.dsems = keep

    def barrier(self):
        for eng in self.ENG:
            for o in self.ENG:
                if o != eng:
                    self._wait(eng, self.sem[o], o, self.cnt[o])
            for b in self.dsems:
                self._wait(eng, b.dsem, id(b.dsem), b.dcnt)


class Alloc:
    def __init__(self, nc):
        self.nc = nc
        self.n = 0
        self.stack = None

    def begin(self):
        self.stack = contextlib.ExitStack()

    def end(self):
        self.stack.close()
        self.stack = None

    def sb(self, shape, dt, name=None):
        self.n += 1
        nm = f"{name or 't'}_{self.n}"
        if self.stack is None:
            return self.nc.alloc_sbuf_tensor(nm, list(shape), dt).ap()
        t = self.stack.enter_context(self.nc.sbuf_tensor(nm, list(shape), dt))
        return t.ap()

    def ps(self, shape, dt=F32, name=None):
        self.n += 1
        return self.nc.alloc_psum_tensor(f"{name or 'p'}_{self.n}", list(shape), dt).ap()


def make_banks(A):
    return [Buf(A.ps([128, 512]), f"bank{i}", excl="self") for i in range(8)]


class FfnCtx:
    def __init__(self, nc, S, A, banks):
        self.nc, self.S, self.A = nc, S, A
        self.X = A.sb([128, NT, D], F32, "X")
        self.Xb = [Buf(self.X[:, t, :], f"X{t}") for t in range(NT)]
        self.nT = A.sb([128, KC, TOK], BF16, "nT")
        self.nTb = [Buf(self.nT[:, :, tb * 512:(tb + 1) * 512], f"nT{tb}") for tb in range(4)]
        nfp = max(PARTS)
        self.act = A.sb([128, nfp, TOK], BF16, "act")
        self.actb = [Buf(self.act[:, :, tb * 512:(tb + 1) * 512], f"act{tb}") for tb in range(4)]
        self.wd = [A.sb([128, nfp, D], BF16, "wd") for _ in range(2)]
        self.wdb = [Buf(w, "wd") for w in self.wd]
        self.wg = [A.sb([128, KC, 256], BF16, "wg") for _ in range(2)]
        self.wu = [A.sb([128, KC, 256], BF16, "wu") for _ in range(2)]
        self.wgb = [Buf(w, "wg") for w in self.wg]
        self.wub = [Buf(w, "wu") for w in self.wu]
        self.wrep = [Buf(A.sb([128, D], F32, "wrep"), "wrep") for _ in range(2)]
        self.nb = [Buf(A.sb([128, D], BF16, "nb"), "nb") for _ in range(4)]
        self.junk = Buf(A.sb([128, D], F32, "junk"), "junk")
        self.ssall = Buf(A.sb([128, NT], F32, "ssall"), "ssall")
        self.sg = [Buf(A.sb([128, 512], F32, "sg"), "sg") for _ in range(2)]
        self.ident = Buf(A.sb([128, 128], BF16, "ident"), "ident")
        self.PG = banks[0:2]
        self.PU = banks[2:4]
        self.PD = banks[4:6]
        self.PT = [Buf(banks[4 + i].ap.bitcast(BF16)[:, 0:KC * 128].rearrange("p (c t) -> p c t", c=KC), "pt",
                       excl=banks[4 + i]) for i in range(4)]
        self.i_n = 0
        self.i_g = 0
        self.i_d = 0
        self.i_w = 0
        self.i_wd = 0


def emit_rstd(C):
    S = C.S
    for tt in range(NT):
        xb = C.Xb[tt]
        S.op("act", lambda e: e.activation(out=C.junk[:], in_=xb[:], func=AF.Square, accum_out=C.ssall[:, tt:tt + 1]),
             [xb], [C.junk, C.ssall])
    S.op("act", lambda e: e.activation(out=C.ssall[:], in_=C.ssall[:], func=AF.Sqrt, scale=1.0 / D, bias=EPS),
         [C.ssall], [C.ssall])
    S.op("dve", lambda e: e.reciprocal(out=C.ssall[:], in_=C.ssall[:]), [C.ssall], [C.ssall])


def emit_norm_T(C, wrep, dst_write):
    S = C.S
    emit_rstd(C)
    for tt in range(NT):
        i = C.i_n
        C.i_n += 1
        nb, pt = C.nb[i % 4], C.PT[i % 4]
        xb = C.Xb[tt]
        S.op("dve", lambda e: e.scalar_tensor_tensor(out=nb[:], in0=xb[:], scalar=C.ssall[:, tt:tt + 1], in1=wrep[:],
                                                     op0=ALU.mult, op1=ALU.mult), [xb, C.ssall, wrep], [nb])
        for c in range(KC):
            S.op("pe", lambda e: e.transpose(pt[:, c, :], nb[:, c * 128:(c + 1) * 128], C.ident[:]),
                 [nb, C.ident], [pt], acc=(c > 0))
        dst_write(tt, pt, "act" if tt % 2 == 0 else "dve")


def emit_ffn(C, wrep, wg_d, wu_d, wd_d):
    S = C.S

    def evac(tt, pt, eng):
        tb = tt // 4
        if eng == "act":
            S.op("act", lambda e: e.copy(out=C.nT[:, :, tt * 128:(tt + 1) * 128], in_=pt[:]), [pt], [C.nTb[tb]])
        else:
            S.op("dve", lambda e: e.tensor_copy(out=C.nT[:, :, tt * 128:(tt + 1) * 128], in_=pt[:]), [pt], [C.nTb[tb]])

    emit_norm_T(C, wrep, evac)
    wg_v = wg_d.ap.rearrange("(c p) n -> p c n", p=128)
    wu_v = wu_d.ap.rearrange("(c p) n -> p c n", p=128)
    wd_v = wd_d.ap.rearrange("(f p) n -> p f n", p=128)
    f0 = 0
    for pi, nfp in enumerate(PARTS):
        wdi = C.i_wd % 2
        C.i_wd += 1
        S.dma("pool", C.wdb[wdi], C.wd[wdi][:, 0:nfp, :], wd_d, wd_v[:, f0:f0 + nfp, :])
        fl = 0
        while fl < nfp:
            gs = min(2, nfp - fl)
            wi = C.i_w % 2
            C.i_w += 1
            cs = (f0 + fl) * 128
            S.dma("pool", C.wgb[wi], C.wg[wi][:, :, 0:gs * 128], wg_d, wg_v[:, :, cs:cs + gs * 128])
            S.dma("pool", C.wub[wi], C.wu[wi][:, :, 0:gs * 128], wu_d, wu_v[:, :, cs:cs + gs * 128])
            for g in range(gs):
                for tb in range(4):
                    i = C.i_g
                    C.i_g += 1
                    pg, pu, sg = C.PG[i % 2], C.PU[i % 2], C.sg[i % 2]
                    for c in range(KC):
                        S.op("pe", lambda e: e.matmul(pg[:], lhsT=C.wg[wi][:, c, g * 128:(g + 1) * 128],
                                                      rhs=C.nT[:, c, tb * 512:(tb + 1) * 512],
                                                      start=(c == 0), stop=(c == KC - 1)),
                             [C.wgb[wi], C.nTb[tb]], [pg], acc=(c > 0))
                    for c in range(KC):
                        S.op("pe", lambda e: e.matmul(pu[:], lhsT=C.wu[wi][:, c, g * 128:(g + 1) * 128],
                                                      rhs=C.nT[:, c, tb * 512:(tb + 1) * 512],
                                                      start=(c == 0), stop=(c == KC - 1)),
                             [C.wub[wi], C.nTb[tb]], [pu], acc=(c > 0))
                    S.op("act", lambda e: e.activation(out=sg[:], in_=pg[:], func=AF.Silu), [pg], [sg])
                    S.op("dve", lambda e: e.tensor_tensor(out=C.act[:, fl + g, tb * 512:(tb + 1) * 512],
                                                          in0=pu[:], in1=sg[:], op=ALU.mult),
                         [pu, sg], [C.actb[tb]])
            fl += gs
        for tt in range(NT):
            tb = tt // 4
            for dh in range(2):
                j = C.i_d
                C.i_d += 1
                pd = C.PD[j % 2]
                for f in range(nfp):
                    S.op("pe", lambda e: e.matmul(pd[:], lhsT=C.act[:, f, tt * 128:(tt + 1) * 128],
                                                  rhs=C.wd[wdi][:, f, dh * 512:(dh + 1) * 512],
                                                  start=(f == 0), stop=(f == nfp - 1)),
                         [C.actb[tb], C.wdb[wdi]], [pd], acc=(f > 0))
                xs = C.X[:, tt, dh * 512:(dh + 1) * 512]
                S.op("dve", lambda e: e.scalar_tensor_tensor(out=xs, in0=pd[:], scalar=0.5, in1=xs,
                                                             op0=ALU.mult, op1=ALU.add),
                     [pd, C.Xb[tt]], [C.Xb[tt]])
        f0 += nfp


def load_X(C, x_d):
    xv = x_d.ap.rearrange("(t p) d -> p t d", p=128)
    for t in range(NT):
        C.S.dma("sp", C.Xb[t], C.X[:, t, :], x_d, xv[:, t, :])


def emit_A(nc, S, A, banks, io):
    C = FfnCtx(nc, S, A, banks)
    S.dma("pool", C.ident, C.ident[:], io["id_d"], io["id_d"][:, :])
    S.dma("sp", C.wrep[0], C.wrep[0][:], io["n1_d"], io["n1_d"][:, :])
    S.dma("sp", C.wrep[1], C.wrep[1][:], io["nm_d"], io["nm_d"][:, :])
    load_X(C, io["x_d"])
    emit_ffn(C, C.wrep[0], io["wg_d"], io["wu_d"], io["wd_d"])
    x1_d = io["x1_d"]
    x1v = x1_d.ap.rearrange("(t p) d -> p t d", p=128)
    for t in range(NT):
        S.dma("sp", x1_d, x1v[:, t, :], C.Xb[t], C.X[:, t, :])

    def evac(tt, pt, eng):
        tb = tt // 4
        if eng == "act":
            S.op("act", lambda e: e.copy(out=C.nT[:, :, tt * 128:(tt + 1) * 128], in_=pt[:]), [pt], [C.nTb[tb]])
        else:
            S.op("dve", lambda e: e.tensor_copy(out=C.nT[:, :, tt * 128:(tt + 1) * 128], in_=pt[:]), [pt], [C.nTb[tb]])
        if tt % 4 == 3:
            io["store_hT"](C, tb)

    emit_norm_T(C, C.wrep[1], evac)
    return C


def build_A():
    nc = bass.Bass("TRN2", target_bir_lowering=False)
    S = Sched(nc)
    A = Alloc(nc)
    banks = make_banks(A)
    io = {
        "x_d": Buf(nc.dram_tensor("x", [TOK, D], F32, kind="ExternalInput").ap(), "x"),
        "n1_d": Buf(nc.dram_tensor("n1", [128, D], F32, kind="ExternalInput").ap()),
        "nm_d": Buf(nc.dram_tensor("nm", [128, D], F32, kind="ExternalInput").ap()),
        "wg_d": Buf(nc.dram_tensor("wg", [D, FF], F32, kind="ExternalInput").ap()),
        "wu_d": Buf(nc.dram_tensor("wu", [D, FF], F32, kind="ExternalInput").ap()),
        "wd_d": Buf(nc.dram_tensor("wd", [FF, D], F32, kind="ExternalInput").ap()),
        "id_d": Buf(nc.dram_tensor("ident", [128, 128], F32, kind="ExternalInput").ap()),
        "x1_d": Buf(nc.dram_tensor("x1", [TOK, D], F32, kind="ExternalOutput").ap()),
    }
    hT_d = Buf(nc.dram_tensor("hT", [D, TOK], BF16, kind="ExternalOutput").ap())

    def store_hT(C, tb):
        hv = hT_d.ap.rearrange("(c p) t -> p c t", p=128)
        S.dma("sp", hT_d, hv[:, :, tb * 512:(tb + 1) * 512], C.nTb[tb], C.nT[:, :, tb * 512:(tb + 1) * 512])

    io["store_hT"] = store_hT
    emit_A(nc, S, A, banks, io)
    S.wait_dma("sp", [io["x1_d"], hT_d])
    return nc


def emit_C(nc, S, A, banks, io):
    C = FfnCtx(nc, S, A, banks)
    S.dma("pool", C.ident, C.ident[:], io["id_d"], io["id_d"][:, :])
    S.dma("sp", C.wrep[0], C.wrep[0][:], io["n2_d"], io["n2_d"][:, :])
    S.dma("sp", C.wrep[1], C.wrep[1][:], io["nf_d"], io["nf_d"][:, :])
    load_X(C, io["x_d"])
    io["load_oT"](C)
    wo_d = io["wo_d"]
    for c in range(KC):
        r0 = io["wo_rows"](c)
        S.dma("pool", C.wdb[c // 4], C.wd[c // 4][:, c % 4, :], wo_d, wo_d[r0:r0 + 128, :])
    for tt in range(NT):
        tb = tt // 4
        for dh in range(2):
            j = C.i_d
            C.i_d += 1
            pd = C.PD[j % 2]
            for c in range(KC):
                S.op("pe", lambda e: e.matmul(pd[:], lhsT=C.nT[:, c, tt * 128:(tt + 1) * 128],
                                              rhs=C.wd[c // 4][:, c % 4, dh * 512:(dh + 1) * 512],
                                              start=(c == 0), stop=(c == KC - 1)),
                     [C.nTb[tb], C.wdb[c // 4]], [pd], acc=(c > 0))
            xs = C.X[:, tt, dh * 512:(dh + 1) * 512]
            S.op("dve", lambda e: e.tensor_tensor(out=xs, in0=pd[:], in1=xs, op=ALU.add),
                 [pd, C.Xb[tt]], [C.Xb[tt]])
    emit_ffn(C, C.wrep[0], io["wg_d"], io["wu_d"], io["wd_d"])
    out_d = io["out_d"]
    ov2 = out_d.ap.rearrange("(t p) d -> p t d", p=128)
    emit_rstd(C)
    for tt in range(NT):
        xb = C.Xb[tt]
        S.op("dve", lambda e: e.scalar_tensor_tensor(out=xb[:], in0=xb[:], scalar=C.ssall[:, tt:tt + 1], in1=C.wrep[1][:],
                                                   op0=ALU.mult, op1=ALU.mult), [xb, C.ssall, C.wrep[1]], [xb])
        S.dma("sp", out_d, ov2[:, tt, :], xb, C.X[:, tt, :])
    return C


def build_C():
    nc = bass.Bass("TRN2", target_bir_lowering=False)
    S = Sched(nc)
    A = Alloc(nc)
    banks = make_banks(A)
    oT_d = Buf(nc.dram_tensor("oT", [D, TOK], BF16, kind="ExternalInput").ap())
    io = {
        "x_d": Buf(nc.dram_tensor("x1", [TOK, D], F32, kind="ExternalInput").ap(), "x1"),
        "wo_d": Buf(nc.dram_tensor("wo", [D, D], F32, kind="ExternalInput").ap()),
        "n2_d": Buf(nc.dram_tensor("n2", [128, D], F32, kind="ExternalInput").ap()),
        "nf_d": Buf(nc.dram_tensor("nf", [128, D], F32, kind="ExternalInput").ap()),
        "wg_d": Buf(nc.dram_tensor("wg", [D, FF], F32, kind="ExternalInput").ap()),
        "wu_d": Buf(nc.dram_tensor("wu", [D, FF], F32, kind="ExternalInput").ap()),
        "wd_d": Buf(nc.dram_tensor("wd", [FF, D], F32, kind="ExternalInput").ap()),
        "id_d": Buf(nc.dram_tensor("ident", [128, 128], F32, kind="ExternalInput").ap()),
        "out_d": Buf(nc.dram_tensor("out", [TOK, D], F32, kind="ExternalOutput").ap()),
        "wo_rows": lambda c: c * 128,
    }

    def load_oT(C):
        ov = oT_d.ap.rearrange("(c p) t -> p c t", p=128)
        for tb in range(4):
            S.dma("sp", C.nTb[tb], C.nT[:, :, tb * 512:(tb + 1) * 512], oT_d, ov[:, :, tb * 512:(tb + 1) * 512])

    io["load_oT"] = load_oT
    emit_C(nc, S, A, banks, io)
    S.wait_dma("sp", [io["out_d"]])
    return nc


class Ring:
    def __init__(self, bufs):
        self.bufs = bufs
        self.i = 0

    def get(self):
        b = self.bufs[self.i % len(self.bufs)]
        self.i += 1
        return b


NB = SEQ // 512
NCH = SEQ // 128
BIG = 1.0e5
C_ID, C_ONES, C_TRIU, C_TRIL, C_MF, C_MB, C_OFFD, C_BLK = range(8)


def host_consts():
    i = np.arange(128)[:, None]
    j = np.arange(128)[None, :]
    c = np.zeros((8, 128, 128), np.float32)
    c[C_ID] = (i == j)
    c[C_ONES] = 1.0
    c[C_TRIU] = (i <= j)
    c[C_TRIL] = (i >= j)
    c[C_MF] = np.where(j <= i, 0.0, BIG)
    c[C_MB] = np.where(j >= i, 0.0, BIG)
    c[C_OFFD] = (i != j)
    c[C_BLK] = ((i // 64) == (j // 64))
    return np.ascontiguousarray(c.transpose(1, 0, 2))


def emit_G(nc, S, A, banks, io, nsteps=NCH, plimit=9):
    w_d, cw_d, par_d, gw_d, cst_d = io["gw_d"], io["cw_d"], io["par_d"], io["gnw_d"], io["cst_d"]

    cst = Buf(A.sb([128, 8, 128], F32, "cst"))
    cstb = Buf(A.sb([128, 8, 128], BF16, "cstb"))
    R4 = A.sb([128, KC * 516], BF16, "R4")
    W = Buf(R4.rearrange("p (c n) -> p c n", c=KC))
    cw = Buf(A.sb([128, 15], F32, "cw"))
    par = Buf(A.sb([128, 4], F32, "par"))
    gw = Buf(A.sb([128, 128], F32, "gw"))
    dg = Buf(A.sb([128, 15, 128], BF16, "dg"))
    QaT = A.sb([128, SEQ], BF16, "QaT")
    KaT = A.sb([128, SEQ], BF16, "KaT")
    VaT = A.sb([128, SEQ], BF16, "VaT")
    QaTb = [Buf(QaT[:, b * 512:(b + 1) * 512]) for b in range(NB)]
    KaTb = [Buf(KaT[:, b * 512:(b + 1) * 512]) for b in range(NB)]
    VaTb = [Buf(VaT[:, b * 512:(b + 1) * 512]) for b in range(NB)]
    Ktok = A.sb([128, NCH, 128], BF16, "Ktok")
    Vtok = A.sb([128, NCH, 128], BF16, "Vtok")
    Ktokb = None
    Vtokb = None
    sz = A.sb([128, NCH, 128], BF16, "sz")
    szb = [Buf(sz[:, c, :]) for c in range(NCH)]
    ab = Buf(A.sb([128, 4, NCH], F32, "ab"))
    R1 = A.sb([128, 3 * 8200], BF16, "R1")
    pre = [R1[:, m * 8200:m * 8200 + SEQ + 4] for m in range(3)]
    oacc = R1.bitcast(F32)[:, 0:SEQ].rearrange("p (c d) -> p c d", d=128)
    oaccb = [Buf(oacc[:, c, :]) for c in range(NCH)]
    preb = [[Buf(pre[m][:, 2 + b * 512: 2 + (b + 1) * 512]) for b in range(NB)] for m in range(3)]
    prepad = [Buf(pre[m][:, 0:2]) for m in range(3)] + [Buf(pre[m][:, SEQ + 2:SEQ + 4]) for m in range(3)]
    R2 = A.sb([128, 2 * KC * 512], BF16, "R2")
    hblk = [Buf(R2[:, i * KC * 512:(i + 1) * KC * 512].rearrange("p (c t) -> p c t", c=KC)) for i in range(2)]

    def cf(k):
        return cst[:, k, :]

    def cb(k):
        return cstb[:, k, :]

    S.dma("sp", cst, cst[:], cst_d, cst_d[:, :, :])
    S.dma("pool", cstb, cstb[:], cst_d, cst_d[:, :, :])
    S.dma("pool", W, W[:], w_d, w_d.ap.rearrange("(c p) n -> p c n", p=128))
    S.dma("sp", cw, cw[:], cw_d, cw_d[:, :])
    S.dma("sp", par, par[:], par_d, par_d[:, :])
    S.dma("sp", gw, gw[:], gw_d, gw_d[:, :])
    for m in range(3):
        S.op("pool", lambda e: e.memset(pre[m][:, 0:2], 0.0), [], [prepad[m]])
        S.op("pool", lambda e: e.memset(pre[m][:, SEQ + 2:SEQ + 4], 0.0), [], [prepad[3 + m]])
    for k in range(15):
        S.op("dve", lambda e: e.tensor_scalar(out=dg[:, k, :], in0=cf(C_ID), scalar1=cw[:, k:k + 1], scalar2=None,
                                              op0=ALU.mult), [cst, cw], [dg])

    pbr = Ring(banks[0:8])
    psr = Ring([Buf(banks[4 + (i % 4)][:, (i // 4) * 128:(i // 4 + 1) * 128], excl=banks[4 + (i % 4)]) for i in range(16)])

    if plimit <= 0:
        S.barrier()
        return
    R3 = A.sb([128, 4096], BF16, "R3")
    R3f = R3.bitcast(F32)
    tmp512 = Ring([Buf(R3f[:, i * 512:(i + 1) * 512]) for i in range(4)])
    for bi, b in enumerate([r_ * 4 + bb_ for bb_ in range(4) for r_ in range(4)]):
        hb = hblk[bi % 2]
        io["load_h"](b, hb)
        for m in range(3):
            p = pbr.get()
            for c in range(KC):
                S.op("pe", lambda e: e.matmul(p[:], lhsT=W[:, c, m * 128:(m + 1) * 128], rhs=hb[:, c, :],
                                              start=(c == 0), stop=(c == KC - 1)), [W, hb], [p], acc=(c > 0))
            S.op("act", lambda e: e.copy(out=pre[m][:, 2 + b * 512:2 + (b + 1) * 512], in_=p[:]), [p], [preb[m][b]])
        for t in range(4):
            ch = b * 4 + t
            p = pbr.get()
            for c in range(KC):
                S.op("pe", lambda e: e.matmul(p[:, 0:132], lhsT=hb[:, c, t * 128:(t + 1) * 128], rhs=W[:, c, 384:516],
                                              start=(c == 0), stop=(c == KC - 1)), [W, hb], [p], acc=(c > 0))
            S.op("act", lambda e: e.activation(out=sz[:, ch, :], in_=p[:, 0:128], func=AF.Silu), [p], [szb[ch]])
            S.op("dve", lambda e: e.tensor_copy(out=ab[:, :, ch], in_=p[:, 128:132]), [p], [ab])

    if plimit <= 1:
        S.barrier()
        return
    S.barrier()
    R2f_ = R2.bitcast(F32)
    NSET = 3
    sT = [Buf(R2f_[:, i * 512:(i + 1) * 512]) for i in range(6)]
    rT = [Buf(R3f[:, i * 512:(i + 1) * 512]) for i in range(4)]
    qT_ = [Buf(R4[:, i * 512:(i + 1) * 512]) for i in range(6)]
    dk_scale = 128.0
    trb = Ring([Buf(banks[7].ap.bitcast(BF16)[:, i * 256:i * 256 + 128], excl=banks[7]) for i in range(4)])

    def conv_main(b):
        nbrs = [bb for bb in (b - 1, b, b + 1) if 0 <= bb < NB]
        for m in range(3):
            p = banks[3 * (b % 2) + m]
            rd = [preb[m][bb] for bb in nbrs] + [dg] + ([prepad[m]] if b == 0 else []) + ([prepad[3 + m]] if b == NB - 1 else [])
            for i in range(5):
                S.op("pe", lambda e: e.matmul(p[:], lhsT=dg[:, m * 5 + i, :], rhs=pre[m][:, b * 512 + i: b * 512 + i + 512],
                                              start=(i == 0), stop=(i == 4)), rd, [p], acc=(i > 0))
            if m == 2:
                S.op("act", lambda e: e.activation(out=VaT[:, b * 512:(b + 1) * 512], in_=p[:], func=AF.Silu), [p], [VaTb[b]])
            else:
                s_, q_ = sT[2 * (b % NSET) + m], qT_[2 * (b % NSET) + m]
                S.op("act", lambda e: e.activation(out=s_[:], in_=p[:], func=AF.Silu), [p], [s_])
                S.op("pool", lambda e: e.tensor_tensor(out=q_[:], in0=s_[:], in1=s_[:], op=ALU.mult), [s_], [q_])

    def conv_chain(b):
        for m in range(2):
            s_, q_, r_ = sT[2 * (b % NSET) + m], qT_[2 * (b % NSET) + m], rT[2 * (b % 2) + m]
            p2 = banks[6]
            S.op("pe", lambda e: e.matmul(p2[:], lhsT=cb(C_ONES), rhs=q_[:], start=True, stop=True), [cstb, q_], [p2])
            sc = dk_scale if m == 0 else 1.0
            S.op("act", lambda e: e.activation(out=r_[:], in_=p2[:], func=AF.Sqrt, scale=sc, bias=EPS * sc), [p2], [r_])
            S.op("dve", lambda e: e.reciprocal(out=r_[:], in_=r_[:]), [r_], [r_])
            dst, dstb = (QaT, QaTb) if m == 0 else (KaT, KaTb)
            S.op("dve", lambda e: e.tensor_tensor(out=dst[:, b * 512:(b + 1) * 512], in0=s_[:], in1=r_[:], op=ALU.mult),
                 [s_, r_], [dstb[b]])

    trK = Buf(banks[7].ap.bitcast(BF16)[:, 0:512].rearrange("p (t d) -> p t d", t=4), excl=banks[7])
    trV = Buf(banks[7].ap.bitcast(BF16)[:, 512:1024].rearrange("p (t d) -> p t d", t=4), excl=banks[7])
    KtokB = [Buf(Ktok[:, 4 * b_:4 * b_ + 4, :]) for b_ in range(NB)]
    VtokB = [Buf(Vtok[:, 4 * b_:4 * b_ + 4, :]) for b_ in range(NB)]

    def conv_trans(b):
        first = True
        for (src, srcb, trp) in ((KaT, KaTb, trK), (VaT, VaTb, trV)):
            for t in range(4):
                ch = b * 4 + t
                S.op("pe", lambda e: e.transpose(trp[:, t, :], src[:, ch * 128:(ch + 1) * 128], cb(C_ID)), [srcb[b], cstb], [trp],
                     acc=(not first))
                first = False
        S.op("act", lambda e: e.copy(out=Ktok[:, 4 * b:4 * b + 4, :], in_=trK[:]), [trK], [KtokB[b]])
        S.op("act", lambda e: e.copy(out=Vtok[:, 4 * b:4 * b + 4, :], in_=trV[:]), [trV], [VtokB[b]])

    Ktokb = [KtokB[c_ // 4] for c_ in range(NCH)]
    Vtokb = [VtokB[c_ // 4] for c_ in range(NCH)]
    LB, LC = 2, 3
    for i in range(NB + LC):
        if i < NB:
            conv_main(i)
        if 0 <= i - LB < NB:
            conv_chain(i - LB)
        if 0 <= i - LC < NB:
            conv_trans(i - LC)

    if plimit <= 2:
        S.barrier()
        return
    def small(name, shape=(128, NCH)):
        return Buf(A.sb(list(shape), F32, name))

    g = [small("g0"), small("g1")]
    beta = [small("b0"), small("b1")]
    G = [small("G0"), small("G1")]
    eG = [small("eG0"), small("eG1")]
    eGl = [small("eGl0"), small("eGl1")]
    kd = [small("kd0"), small("kd1")]
    bg = [small("bg0"), small("bg1")]
    nA = small("nA", (128, 2))
    t1, t2, t3 = small("t1"), small("t2"), small("t3")
    S.op("act", lambda e: e.activation(out=nA[:], in_=par[:, 0:2], func=AF.Exp), [par], [nA])
    S.op("dve", lambda e: e.tensor_scalar(out=nA[:], in0=nA[:], scalar1=-1.0, scalar2=None, op0=ALU.mult), [nA], [nA])
    for d in range(2):
        S.op("dve", lambda e: e.tensor_scalar(out=t1[:], in0=ab[:, d, :], scalar1=par[:, 2 + d:3 + d], scalar2=None, op0=ALU.add),
             [ab, par], [t1])
        S.op("dve", lambda e: e.tensor_scalar(out=t2[:], in0=t1[:], scalar1=-1.0, scalar2=None, op0=ALU.mult), [t1], [t2])
        S.op("dve", lambda e: e.tensor_tensor(out=t2[:], in0=t2[:], in1=t1[:], op=ALU.max), [t1, t2], [t2])
        S.op("act", lambda e: e.activation(out=t2[:], in_=t2[:], func=AF.Exp, scale=-1.0), [t2], [t2])
        S.op("act", lambda e: e.activation(out=t2[:], in_=t2[:], func=AF.Ln, bias=1.0), [t2], [t2])
        S.op("dve", lambda e: e.tensor_scalar(out=t1[:], in0=t1[:], scalar1=0.0, scalar2=None, op0=ALU.max), [t1], [t1])
        S.op("dve", lambda e: e.tensor_tensor(out=t1[:], in0=t1[:], in1=t2[:], op=ALU.add), [t1, t2], [t1])
        S.op("dve", lambda e: e.tensor_scalar(out=g[d][:], in0=t1[:], scalar1=nA[:, d:d + 1], scalar2=None, op0=ALU.mult),
             [t1, nA], [g[d]])
        S.op("act", lambda e: e.activation(out=beta[d][:], in_=ab[:, 2 + d, :], func=AF.Sigmoid), [ab], [beta[d]])
        p = psr.get()
        S.op("pe", lambda e: e.matmul(p[:, 0:NCH], lhsT=cf(C_TRIU if d == 0 else C_TRIL), rhs=g[d][:], start=True, stop=True),
             [cst, g[d]], [p])
        S.op("dve", lambda e: e.tensor_copy(out=G[d][:], in_=p[:, 0:NCH]), [p], [G[d]])
        p = psr.get()
        S.op("pe", lambda e: e.matmul(p[:, 0:NCH], lhsT=cf(C_ONES), rhs=g[d][:], start=True, stop=True), [cst, g[d]], [p])
        S.op("dve", lambda e: e.tensor_copy(out=t3[:], in_=p[:, 0:NCH]), [p], [t3])
        S.op("act", lambda e: e.activation(out=eG[d][:], in_=G[d][:], func=AF.Exp), [G[d]], [eG[d]])
        S.op("act", lambda e: e.activation(out=eGl[d][:], in_=t3[:], func=AF.Exp), [t3], [eGl[d]])
        S.op("dve", lambda e: e.tensor_tensor(out=t3[:], in0=t3[:], in1=G[d][:], op=ALU.subtract), [t3, G[d]], [t3])
        S.op("act", lambda e: e.activation(out=kd[d][:], in_=t3[:], func=AF.Exp), [t3], [kd[d]])
        S.op("dve", lambda e: e.tensor_tensor(out=bg[d][:], in0=beta[d][:], in1=eG[d][:], op=ALU.mult), [beta[d], eG[d]], [bg[d]])

    if plimit <= 3:
        S.barrier()
        return
    S.barrier()
    S.op("pool", lambda e: e.memset(oacc[:], 0.0), [], oaccb)
    Sst = [Buf(A.sb([128, 128], F32, "S")) for _ in range(2)]
    Sbf = [Buf(A.sb([128, 128], BF16, "Sb")) for _ in range(2)]
    for d in range(2):
        S.op("pool", lambda e: e.memset(Sst[d][:], 0.0), [], [Sst[d]])
        S.op("pool", lambda e: e.memset(Sbf[d][:], 0.0), [], [Sbf[d]])

    def mm(out, lhsT_ap, rhs_ap, rd):
        S.op("pe", lambda e: e.matmul(out[:], lhsT=lhsT_ap, rhs=rhs_ap, start=True, stop=True), rd, [out])

    def tr(out_bf_ap, out_buf, in_ap, rd):
        S.op("pe", lambda e: e.transpose(out_bf_ap, in_ap, cb(C_ID)), rd + [cstb], [out_buf])

    class Slot:
        pass

    NSLOT = 8
    regions = [(R2, 0), (R2, 4096), (R1, 16384), (R1, 16384 + 4096), (R3, 0), (R4, 0), (VaT, 0), (VaT, 4096)]
    slots = []
    for k in range(NSLOT):
        reg, base_bf = regions[k]
        regf = reg.bitcast(F32)
        fb = base_bf // 2

        def ft(i, n=1, _regf=regf, _fb=fb):
            return Buf(_regf[:, _fb + i * 128:_fb + (i + n) * 128])

        sl = Slot()
        sl.T = [ft(i) for i in range(4)]
        sl.PA, sl.PB, sl.PC = ft(4, 2), ft(6, 2), ft(8, 2)
        sl.RA, sl.RB = ft(10), ft(11)
        sl.UW = ft(12, 2)
        sl.bf = [Buf(reg[:, base_bf + 3584 + i * 128: base_bf + 3584 + (i + 1) * 128]) for i in range(4)]
        bk = banks[k]
        sl.bank = bk
        sl.p01 = Buf(bk[:, 0:256], excl=bk)
        sl.p2 = Buf(bk[:, 256:384], excl=bk)
        sl.p3 = Buf(bk[:, 384:512], excl=bk)
        sl.pq = [Buf(bk[:, i * 128:(i + 1) * 128], excl=bk) for i in range(4)]
        slots.append(sl)
    chain_done = {0: -1, 1: -1}

    def job(s, d, R):
        c = s if d == 0 else NCH - 1 - s
        blk = c // 4
        col = slice(c, c + 1)
        kT = KaT[:, c * 128:(c + 1) * 128]
        qT = QaT[:, c * 128:(c + 1) * 128]
        gb, Dm, E, Li = R.T
        In, InT, Kdec, vn = R.bf
        p01, p2, p3 = R.p01, R.p2, R.p3
        S.op("pool", lambda e: e.tensor_scalar(out=gb[:], in0=cf(C_ONES), scalar1=g[d][:, col], scalar2=None, op0=ALU.mult), [cst, g[d]], [gb])
        yield
        mm(p2, gb[:], cf(C_TRIU if d == 0 else C_TRIL), [gb, cst])
        yield
        S.op("dve", lambda e: e.scalar_tensor_tensor(out=Dm[:], in0=p2[:], scalar=G[d][:, col], in1=cf(C_MF if d == 0 else C_MB),
                                                     op0=ALU.subtract, op1=ALU.add), [p2, G[d], cst], [Dm])
        yield
        S.op("act", lambda e: e.activation(out=E[:], in_=Dm[:], func=AF.Exp, scale=-1.0), [Dm], [E])
        yield
        S.op("pe", lambda e: e.matmul(p01[:, 0:128], lhsT=kT, rhs=kT, start=True, stop=True), [KaTb[blk]], [p01])
        yield
        S.op("pe", lambda e: e.matmul(p01[:, 128:256], lhsT=qT, rhs=kT, start=True, stop=True), [QaTb[blk], KaTb[blk]], [p01])
        yield
        S.op("dve", lambda e: e.scalar_tensor_tensor(out=Li[:], in0=p01[:, 0:128], scalar=beta[d][:, col], in1=E[:],
                                                     op0=ALU.mult, op1=ALU.mult), [p01, beta[d], E], [Li])
        yield
        S.op("dve", lambda e: e.tensor_tensor(out=In[:], in0=p01[:, 128:256], in1=E[:], op=ALU.mult), [p01, E], [In])
        yield
        PA = R.PA
        S.op("pool", lambda e: e.tensor_tensor(out=PA[:, 0:128], in0=Li[:], in1=cf(C_OFFD), op=ALU.mult), [Li, cst], [PA])
        yield
        S.op("pe", lambda e: e.transpose(p2[:], PA[:, 0:128], cf(C_ID)), [PA, cst], [p2])
        yield
        p3v = p3.ap.bitcast(BF16)[:, 0:128]
        tr(p3v, p3, In[:], [In])
        yield
        S.op("act", lambda e: e.copy(out=PA[:, 128:256], in_=p2[:]), [p2], [PA])
        yield
        RT = R.RA
        S.op("dve", lambda e: e.tensor_tensor(out=RT[:], in0=cf(C_ID), in1=p2[:], op=ALU.subtract), [cst, p2], [RT])
        yield
        S.op("act", lambda e: e.copy(out=InT[:], in_=p3v), [p3], [InT])
        yield
        src = PA
        for k in range(1, 7):
            dstp = R.PB if k % 2 == 1 else R.PC
            S.op("pe", lambda e: e.matmul(p01[:, 0:128], lhsT=src[:, 128:256], rhs=src[:, 0:128], start=True, stop=True), [src], [p01])
            yield
            w = 128
            if k < 6:
                S.op("pe", lambda e: e.matmul(p01[:, 128:256], lhsT=src[:, 0:128], rhs=src[:, 128:256], start=True, stop=True), [src], [p01])
                yield
                w = 256
            S.op("act", lambda e: e.copy(out=dstp[:, 0:w], in_=p01[:, 0:w]), [p01], [dstp])
            yield
            mm(p2, dstp[:, 0:128], RT[:], [dstp, RT])
            yield
            RTn = R.RB if RT is R.RA else R.RA
            S.op("dve", lambda e: e.tensor_tensor(out=RTn[:], in0=p2[:], in1=RT[:], op=ALU.add), [p2, RT], [RTn])
            yield
            src, RT = dstp, RTn
        TmT = RT
        Vb32, Kbg32 = gb, Dm
        S.op("pool", lambda e: e.tensor_scalar(out=Vb32[:], in0=Vtok[:, c, :], scalar1=beta[d][:, col], scalar2=None, op0=ALU.mult),
             [Vtokb[c], beta[d]], [Vb32])
        yield
        S.op("pool", lambda e: e.tensor_scalar(out=Kbg32[:], in0=Ktok[:, c, :], scalar1=bg[d][:, col], scalar2=None, op0=ALU.mult),
             [Ktokb[c], bg[d]], [Kbg32])
        yield
        S.op("pool", lambda e: e.tensor_scalar(out=Kdec[:], in0=Ktok[:, c, :], scalar1=kd[d][:, col], scalar2=None, op0=ALU.mult),
             [Ktokb[c], kd[d]], [Kdec])
        yield
        S.op("pe", lambda e: e.matmul(p01[:, 0:128], lhsT=TmT[:], rhs=Vb32[:], start=True, stop=True), [TmT, Vb32], [p01])
        yield
        S.op("pe", lambda e: e.matmul(p01[:, 128:256], lhsT=Kbg32[:], rhs=TmT[:], start=True, stop=True), [Kbg32, TmT], [p01])
        yield
        UW = R.UW
        S.op("act", lambda e: e.copy(out=UW[:], in_=p01[:]), [p01], [UW])
        yield
        while chain_done[d] != s - 1:
            yield
        pWS, pQS, pIV, pKV = R.pq
        mm(pWS, UW[:, 128:256], Sst[d][:], [UW, Sst[d]])
        yield
        mm(pQS, qT, Sbf[d][:], [QaTb[blk], Sbf[d]])
        yield
        S.op("dve", lambda e: e.tensor_tensor(out=vn[:], in0=UW[:, 0:128], in1=pWS[:], op=ALU.subtract), [UW, pWS], [vn])
        yield
        mm(pIV, InT[:], vn[:], [InT, vn])
        yield
        mm(pKV, Kdec[:], vn[:], [Kdec, vn])
        yield
        S.op("dve", lambda e: e.scalar_tensor_tensor(out=Sst[d][:], in0=Sst[d][:], scalar=eGl[d][:, col], in1=pKV[:],
                                                     op0=ALU.mult, op1=ALU.add), [Sst[d], eGl[d], pKV], [Sst[d]])
        yield
        S.op("act", lambda e: e.copy(out=Sbf[d][:], in_=Sst[d][:]), [Sst[d]], [Sbf[d]])
        yield
        S.op("dve", lambda e: e.scalar_tensor_tensor(out=oacc[:, c, :], in0=pQS[:], scalar=eG[d][:, col], in1=oacc[:, c, :],
                                                     op0=ALU.mult, op1=ALU.add), [pQS, eG[d], oaccb[c]], [oaccb[c]])
        yield
        S.op("dve", lambda e: e.tensor_tensor(out=oacc[:, c, :], in0=pIV[:], in1=oacc[:, c, :], op=ALU.add),
             [pIV, oaccb[c]], [oaccb[c]])
        yield
        chain_done[d] = s

    JOBLEN = 72
    NST = NSLOT // 2
    active = []
    nxt = 0
    rounds = 0
    last_launch = -10 ** 9
    while nxt < nsteps or active:
        if nxt < nsteps and len(active) <= NSLOT - 2 and (not active or rounds - last_launch >= JOBLEN // NST):
            for d_ in range(2):
                k_ = 2 * (nxt % NST) + d_
                assert all(k_ != kk for kk, _ in active), "slot still busy"
                active.append((k_, job(nxt, d_, slots[k_])))
            nxt += 1
            last_launch = rounds
        rounds += 1
        still = []
        for k_, gjob in active:
            try:
                next(gjob)
                still.append((k_, gjob))
            except StopIteration:
                pass
        active = still
    r_bf = Ring(slots[0].bf)
    psr = Ring([Buf(banks[6 + (i % 2)][:, (i // 2) * 128:(i // 2 + 1) * 128], excl=banks[6 + (i % 2)]) for i in range(8)])

    S.barrier()
    oT = VaT
    oTb = [Buf(oT[:, b * 512:(b + 1) * 512]) for b in range(NB)]
    junk = Buf(A.sb([128, 128], F32, "junkg"))
    ssall = Buf(A.sb([128, NCH], F32, "ssall"))
    szb_all = Buf(sz)
    S.op("pool", lambda e: e.tensor_tensor(out=sz[:], in0=sz[:], in1=gw[:].unsqueeze(1).broadcast_to([128, NCH, 128]), op=ALU.mult),
         szb + [gw], szb)
    for c in range(NCH):
        S.op("act", lambda e: e.activation(out=junk[:], in_=oacc[:, c, :], func=AF.Square, accum_out=ssall[:, c:c + 1]),
             [oaccb[c]], [junk, ssall])
    S.op("act", lambda e: e.activation(out=ssall[:], in_=ssall[:], func=AF.Sqrt, scale=1.0 / 128, bias=EPS), [ssall], [ssall])
    S.op("dve", lambda e: e.reciprocal(out=ssall[:], in_=ssall[:]), [ssall], [ssall])
    trO = [Buf(banks[6 + i].ap.bitcast(BF16)[:, 0:512].rearrange("p (t d) -> p t d", t=4), excl=banks[6 + i]) for i in range(2)]
    for c in range(NCH):
        ob = r_bf.get()
        S.op("dve", lambda e: e.scalar_tensor_tensor(out=ob[:], in0=oacc[:, c, :], scalar=ssall[:, c:c + 1], in1=sz[:, c, :],
                                                     op0=ALU.mult, op1=ALU.mult), [oaccb[c], ssall, szb[c]], [ob])
        tp = trO[(c // 4) % 2]
        S.op("pe", lambda e: e.transpose(tp[:, c % 4, :], ob[:], cb(C_ID)), [ob, cstb], [tp], acc=(c % 4 != 0))
        if c % 4 == 3:
            b_ = c // 4
            S.op("act", lambda e: e.copy(out=oT[:, b_ * 512:(b_ + 1) * 512].rearrange("p (t d) -> p t d", t=4), in_=tp[:]),
                 [tp], [oTb[b_]])
    io["store_oa"](oT, oTb)


def _ext_mixer_io(nc, S, out_name):
    hT_d = Buf(nc.dram_tensor("hT", [D, SEQ], BF16, kind="ExternalInput").ap())
    oT_d = Buf(nc.dram_tensor(out_name, [128, SEQ], BF16, kind="ExternalOutput").ap())
    hv = hT_d.ap.rearrange("(c p) t -> p c t", p=128)

    def load_h(b, hb):
        S.dma("sp", hb, hb[:], hT_d, hv[:, :, b * 512:(b + 1) * 512])

    def store(oT, oTb, quarter=None):
        for b in (range(NB) if quarter is None else range(4 * quarter, 4 * quarter + 4)):
            S.dma("sp", oT_d, oT_d[:, b * 512:(b + 1) * 512], oTb[b], oT[:, b * 512:(b + 1) * 512])

    return hT_d, oT_d, load_h, store


def build_G(nsteps=NCH, plimit=9):
    nc = bass.Bass("TRN2", target_bir_lowering=False)
    S = Sched(nc)
    A = Alloc(nc)
    banks = make_banks(A)
    hT_d, oT_d, load_h, store = _ext_mixer_io(nc, S, "oaT")
    io = {
        "gw_d": Buf(nc.dram_tensor("w", [D, 516], F32, kind="ExternalInput").ap()),
        "cw_d": Buf(nc.dram_tensor("cw", [128, 15], F32, kind="ExternalInput").ap()),
        "par_d": Buf(nc.dram_tensor("par", [128, 4], F32, kind="ExternalInput").ap()),
        "gnw_d": Buf(nc.dram_tensor("gw", [128, 128], F32, kind="ExternalInput").ap()),
        "cst_d": Buf(nc.dram_tensor("cst", [128, 8, 128], F32, kind="ExternalInput").ap()),
        "load_h": load_h, "store_oa": store,
    }
    emit_G(nc, S, A, banks, io, nsteps, plimit)
    S.wait_dma("sp", [oT_d])
    return nc


def g_inputs(inp, j, hT_full, cst):
    w_in = inp["w_in"][0]
    cols = np.concatenate([np.arange(j * 128, (j + 1) * 128), 512 + np.arange(j * 128, (j + 1) * 128),
                           1024 + np.arange(j * 128, (j + 1) * 128), 1536 + np.arange(j * 128, (j + 1) * 128),
                           np.array([2048 + j, 2052 + j, 2056 + j, 2060 + j])])
    w = np.ascontiguousarray(w_in[:, cols])
    cwf = inp["conv_w"][0]
    cw = np.stack([cwf[m * 512 + j * 128:m * 512 + (j + 1) * 128, :] for m in range(3)], axis=1).reshape(128, 15)
    par = np.array([inp["a_log"][0, 0, j], inp["a_log"][0, 1, j], inp["dt_bias"][0, 0, j], inp["dt_bias"][0, 1, j]], np.float32)
    par = np.ascontiguousarray(np.broadcast_to(par[None, :], (128, 4)))
    gw = np.ascontiguousarray(np.broadcast_to(inp["gdn_norm_w"][0][None, :], (128, 128)))
    return {"hT": hT_full, "w": w, "cw": np.ascontiguousarray(cw), "par": par, "gw": gw, "cst": cst}


DILS = (1, 4, 16)
KPAD = 1024
NEG = -30000.0


def t5_bucket_np(rel):
    nb = 16
    bucket = (rel > 0).astype(np.int32) * nb
    n = np.abs(rel)
    max_exact = nb // 2
    large = max_exact + (np.log(np.maximum(n, 1) / max_exact) / np.log(1024 / max_exact) * (nb - max_exact)).astype(np.int32)
    large = np.minimum(large, nb - 1)
    return (bucket + np.where(n < max_exact, n, large)).astype(np.int32)


def swa_tables(rel_bias, j):
    kk = np.arange(128)[:, None]
    qq = np.arange(128)[None, :]
    bias_g = np.zeros((128, 3 * 2 * 2, 128), np.float32)
    maskc = np.zeros((128, 4, 128), np.float32)
    for pos in range(2):
        rel = kk - 64 - qq if pos == 0 else kk + 64 - qq
        valid = np.abs(rel) <= 64
        for var in range(2):
            v = valid.copy()
            if var == 1:
                if pos == 0:
                    v &= (kk >= 64)
                else:
                    v &= (kk < 64)
            maskc[:, pos * 2 + var, :] = np.where(v, 0.0, NEG)
        for di, dil in enumerate(DILS):
            bk = t5_bucket_np(rel * dil)
            for hd in range(2):
                bias_g[:, (di * 2 + pos) * 2 + hd, :] = rel_bias[bk, 2 * j + hd]
    return bias_g, maskc


def emit_W(nc, S, A, banks, io, npat=3):
    w_d, nw_d, bg_d, mk_d, cst_d = io["ww_d"], io["nw_d"], io["bg_d"], io["mk_d"], io["cst_d"]
    Vd, Rd = io["Vd"], io["Rd"]

    cst = Buf(A.sb([128, 8, 128], F32, "cst"))
    cstb = Buf(A.sb([128, 8, 128], BF16, "cstb"))
    W = Buf(A.sb([128, KC, 384], BF16, "W"))
    nw = Buf(A.sb([128, 2], F32, "nw"))
    tabc = Buf(A.sb([128, 18, 256], F32, "tabc"))
    QbT = A.sb([128, SEQ], BF16, "QbT")
    KbT = A.sb([128, SEQ + 2 * KPAD], BF16, "KbT")
    QbTb = Buf(QbT)
    KbTb = Buf(KbT)
    RW1 = A.sb([128, 2 * KC * 512], BF16, "RW1")
    hblk = [Buf(RW1[:, i * KC * 512:(i + 1) * KC * 512].rearrange("p (c t) -> p c t", c=KC)) for i in range(2)]
    Vst = [Buf(A.sb([128, 4, 130], BF16, "Vst")) for _ in range(2)]
    Te = Buf(A.sb([128, NCH + 1, 130], BF16, "Te"))
    res_ap = A.sb([128, NCH * 130], F32, "res")
    res = Buf(res_ap.rearrange("p (n c) -> p n c", c=130))
    bgt = Buf(res_ap[:, 0:1536].rearrange("p (k c) -> p k c", c=128))
    mkt = Buf(res_ap[:, 1536:2048].rearrange("p (k c) -> p k c", c=128))
    acc = Buf(A.sb([128, NCH, 130], F32, "acc"))
    RW2 = A.sb([128, 10240], BF16, "RW2")
    RW2f = RW2.bitcast(F32)
    tmp512 = Ring([Buf(RW2f[:, i * 512:(i + 1) * 512]) for i in range(4)])
    qfr = Ring([Buf(RW2f[:, (4 + i) * 512:(5 + i) * 512]) for i in range(6)])
    Te2 = Buf(RW2[:, 0:(NCH + 1) * 130].rearrange("p (k c) -> p k c", c=130))
    pbr = Ring(banks[0:4])

    def cf(k):
        return cst[:, k, :]

    def cb(k):
        return cstb[:, k, :]

    S.dma("sp", cst, cst[:], cst_d, cst_d[:, :, :])
    S.dma("pool", cstb, cstb[:], cst_d, cst_d[:, :, :])
    S.dma("pool", W, W[:], w_d, w_d.ap.rearrange("(c p) n -> p c n", p=128))
    S.dma("sp", nw, nw[:], nw_d, nw_d[:, :])
    S.dma("sp", bgt, bgt[:], bg_d, bg_d[:, :, :])
    S.dma("sp", mkt, mkt[:], mk_d, mk_d[:, :, :])
    for di in range(3):
        for hd in range(2):
            for v3, (vlo, vhi) in enumerate(((0, 0), (1, 0), (0, 1))):
                ti = (di * 2 + hd) * 3 + v3
                for pos, var in ((0, vlo), (1, vhi)):
                    S.op("pool", lambda e: e.tensor_tensor(out=tabc[:, ti, pos * 128:(pos + 1) * 128],
                                                           in0=bgt[:, (di * 2 + pos) * 2 + hd, :],
                                                           in1=mkt[:, pos * 2 + var, :], op=ALU.add), [bgt, mkt], [tabc])
    S.op("pool", lambda e: e.memset(KbT[:, 0:KPAD], 0.0), [], [KbTb])
    S.op("pool", lambda e: e.memset(KbT[:, KPAD + SEQ:], 0.0), [], [KbTb])
    for i in range(2):
        S.op("pool", lambda e: e.memset(Vst[i][:], 1.0), [], [Vst[i]])
    S.op("pool", lambda e: e.memset(Te[:], 0.0), [], [Te])
    if "after_setup" in io:
        io["after_setup"]()

    Vdv = Vd.ap.rearrange("(t p) c -> p t c", p=128)
    st = {}

    def main(b):
        hb = hblk[b % 2]
        io["load_h"](b, hb)
        bq, bk, bv = banks[4 * (b % 2) + 0], banks[4 * (b % 2) + 1], banks[4 * (b % 2) + 2]
        for m, p in ((0, bq), (1, bk)):
            for c in range(KC):
                S.op("pe", lambda e: e.matmul(p[:], lhsT=W[:, c, m * 128:(m + 1) * 128], rhs=hb[:, c, :],
                                              start=(c == 0), stop=(c == KC - 1)), [W, hb], [p], acc=(c > 0))
        for t in range(4):
            for c in range(KC):
                S.op("pe", lambda e: e.matmul(bv[:, t * 128:(t + 1) * 128], lhsT=hb[:, c, t * 128:(t + 1) * 128], rhs=W[:, c, 256:384],
                                              start=(c == 0), stop=(c == KC - 1)), [W, hb], [bv], acc=(c > 0 or t > 0))
        sqs = []
        for m, p in ((0, bq), (1, bk)):
            sq = tmp512.get()
            qf = qfr.get()
            S.op("act", lambda e: e.activation(out=sq[:], in_=p[:], func=AF.Square), [p], [sq])
            S.op("act", lambda e: e.copy(out=qf[:], in_=p[:]), [p], [qf])
            sqs.append((sq, qf))
        vs = Vst[b % 2]
        S.op("act", lambda e: e.copy(out=vs[:].rearrange("p t (h c) -> p t h c", h=2)[:, :, :, 0:64],
                                     in_=bv[:].rearrange("p (t h c) -> p t h c", t=4, h=2)), [bv], [vs])
        S.dma("act", Vd, Vdv[:, b * 4:(b + 1) * 4, :], vs, vs[:])
        st[b] = sqs

    def chain(b):
        bs = banks[4 * (b % 2) + 3]
        sqs = st.pop(b)
        for m in range(2):
            sq, p = sqs[m]
            S.op("pe", lambda e: e.matmul(bs[:], lhsT=cf(C_BLK), rhs=sq[:], start=True, stop=True), [cst, sq], [bs])
            if m == 0:
                S.op("act", lambda e: e.activation(out=sq[:], in_=bs[:], func=AF.Sqrt, scale=1.0, bias=64.0 * EPS), [bs], [sq])
            else:
                S.op("act", lambda e: e.activation(out=sq[:], in_=bs[:], func=AF.Sqrt, scale=1.0 / 64, bias=EPS), [bs], [sq])
            S.op("dve", lambda e: e.reciprocal(out=sq[:], in_=sq[:]), [sq], [sq])
            dst = QbT[:, b * 512:(b + 1) * 512] if m == 0 else KbT[:, KPAD + b * 512:KPAD + (b + 1) * 512]
            S.op("dve", lambda e: e.scalar_tensor_tensor(out=dst, in0=p[:], scalar=nw[:, m:m + 1], in1=sq[:],
                                                         op0=ALU.mult, op1=ALU.mult), [p, nw, sq], [QbTb if m == 0 else KbTb])

    for b in range(NB + 1):
        if b < NB:
            main(b)
        if b >= 1:
            chain(b - 1)

    S.barrier()
    psr = Ring([Buf(banks[i % 4][:, (i // 4) * 256:(i // 4 + 1) * 256], excl=banks[i % 4]) for i in range(8)])
    psr1 = Ring([Buf(banks[i % 4][:, (i // 4) * 128:(i // 4 + 1) * 128], excl=banks[i % 4]) for i in range(16)])
    por = Ring([banks[4], banks[5], banks[6], banks[7]])
    sbr = Ring([Buf(A.sb([128, 256], F32, "sb")) for _ in range(4)])
    ppr = Ring([Buf(A.sb([128, 256], BF16, "pp")) for _ in range(8)])
    S.op("pool", lambda e: e.memset(Te2[:], 0.0), [], [Te2])
    Rd2 = io["Rd2"]
    TeB = [Te, Te2]

    def load_T(di, Tb):
        dil = DILS[di]
        nbs = (SEQ // dil) // 128
        for r in range(dil):
            seg = Vd.ap.rearrange("(t d) c -> d t c", d=dil)[r]
            segv = seg.rearrange("(k i) c -> i k c", i=128)
            k0 = r * nbs
            step = min(nbs, 8)
            for ks in range(0, nbs, step):
                S.dma("sp", Tb, Tb[64:128, k0 + ks:k0 + ks + step, :], Vd, segv[0:64, ks:ks + step, :])
                S.dma("sp", Tb, Tb[0:64, k0 + 1 + ks:k0 + 1 + ks + step, :], Vd, segv[64:128, ks:ks + step, :])

    order = [1, 2, 0]
    load_T(order[0], TeB[0])
    for idx, di in enumerate(order):
        dil = DILS[di]
        L = SEQ // dil
        nbs = L // 128
        Te = TeB[idx % 2]
        if idx + 1 < 3:
            load_T(order[idx + 1], TeB[(idx + 1) % 2])
        dst = (res, acc, res)[idx]
        Rdx = (Rd, Rd2, None)[idx]
        items = [(n, hd) for n in range(NCH) for hd in range(2)]
        LAG = 4
        pend = {}

        def stage1(n, hd):
            r = (128 * n) // L
            t0 = 128 * n - r * L
            hs = slice(hd * 64, (hd + 1) * 64)
            q0 = r + dil * t0
            qT = QbT[hs, q0:q0 + dil * 127 + 1:dil]
            v3 = 1 if t0 == 0 else (2 if t0 + 128 == L else 0)
            ti = (di * 2 + hd) * 3 + v3
            ps_ = psr.get()
            for pos in range(2):
                kk0 = KPAD + r + dil * (t0 - 64 + 128 * pos)
                kT = KbT[hs, kk0:kk0 + dil * 127 + 1:dil]
                S.op("pe", lambda e: e.matmul(ps_[:, pos * 128:(pos + 1) * 128], lhsT=kT, rhs=qT, start=True, stop=True),
                     [KbTb, QbTb], [ps_])
            sb_ = sbr.get()
            S.op("dve", lambda e: e.tensor_tensor(out=sb_[:], in0=ps_[:], in1=tabc[:, ti, :], op=ALU.add), [ps_, tabc], [sb_])
            pp_ = ppr.get()
            S.op("act", lambda e: e.activation(out=pp_[:], in_=sb_[:], func=AF.Exp), [sb_], [pp_])
            pend[(n, hd)] = pp_

        def stage2(n, hd):
            pp_ = pend.pop((n, hd))
            po = por.get()
            for pos in range(2):
                S.op("pe", lambda e: e.matmul(po[:, 0:65], lhsT=pp_[:, pos * 128:(pos + 1) * 128], rhs=Te[:, n + pos, hd * 65:(hd + 1) * 65],
                                              start=(pos == 0), stop=(pos == 1)), [pp_, Te], [po], acc=(pos == 1))
            S.op("act", lambda e: e.copy(out=dst[:, n, hd * 65:(hd + 1) * 65], in_=po[:, 0:65]), [po], [dst])

        for k in range(len(items) + LAG):
            if k < len(items):
                stage1(*items[k])
            if k >= LAG:
                stage2(*items[k - LAG])
        if Rdx is not None:
            for r in range(dil):
                seg = Rdx.ap.rearrange("(t d) c -> d t c", d=dil)[r]
                segv = seg.rearrange("(k i) c -> i k c", i=128)
                k0 = r * nbs
                S.dma("sp", Rdx, segv[:, :, :], dst, dst[:, k0:k0 + nbs, :])
    for Rdx in (Rd2, Rd):
        Rdv = Rdx.ap.rearrange("(n p) c -> p n c", p=128)
        for q4 in range(4):
            S.dma("sp", acc, acc[:, q4 * 16:(q4 + 1) * 16, :], Rdx, Rdv[:, q4 * 16:(q4 + 1) * 16, :])
        S.op("dve", lambda e: e.tensor_tensor(out=res[:], in0=res[:], in1=acc[:], op=ALU.add), [acc, res], [res])
    acc = res

    rden = Buf(A.sb([128, NCH, 2], F32, "rden"))
    S.op("dve", lambda e: e.reciprocal(out=rden[:], in_=acc[:].rearrange("p n (h c) -> p n h c", h=2)[:, :, :, 64]), [acc], [rden])
    oT = RW1
    oTb = [Buf(oT[:, b * 512:(b + 1) * 512]) for b in range(NB)]
    accv = acc[:].rearrange("p n (h c) -> p n h c", h=2)
    trO = [Buf(banks[i].ap.bitcast(BF16)[:, 0:512].rearrange("p (t d) -> p t d", t=4), excl=banks[i]) for i in range(2)]
    for n in range(NCH):
        ob = ppr.get()
        S.op("dve", lambda e: e.tensor_tensor(out=ob[:, 0:128].rearrange("p (h c) -> p h c", h=2), in0=accv[:, n, :, 0:64],
                                              in1=rden[:, n, :].unsqueeze(2).broadcast_to([128, 2, 64]), op=ALU.mult),
             [acc, rden], [ob])
        tp = trO[(n // 4) % 2]
        S.op("pe", lambda e: e.transpose(tp[:, n % 4, :], ob[:, 0:128], cb(C_ID)), [ob, cstb], [tp], acc=(n % 4 != 0))
        if n % 4 == 3:
            b_ = n // 4
            S.op("act", lambda e: e.copy(out=oT[:, b_ * 512:(b_ + 1) * 512].rearrange("p (t d) -> p t d", t=4), in_=tp[:]),
                 [tp], [oTb[b_]])
        if n % 16 == 15:
            io["store_ob"](oT, oTb, n // 16)


def build_W(npat=3):
    nc = bass.Bass("TRN2", target_bir_lowering=False)
    S = Sched(nc)
    A = Alloc(nc)
    banks = make_banks(A)
    hT_d, oT_d, load_h, store = _ext_mixer_io(nc, S, "obT")
    io = {
        "ww_d": Buf(nc.dram_tensor("w", [D, 384], F32, kind="ExternalInput").ap()),
        "nw_d": Buf(nc.dram_tensor("nw", [128, 2], F32, kind="ExternalInput").ap()),
        "bg_d": Buf(nc.dram_tensor("bias_g", [128, 12, 128], F32, kind="ExternalInput").ap()),
        "mk_d": Buf(nc.dram_tensor("maskc", [128, 4, 128], F32, kind="ExternalInput").ap()),
        "cst_d": Buf(nc.dram_tensor("cst", [128, 8, 128], F32, kind="ExternalInput").ap()),
        "Vd": Buf(nc.dram_tensor("Vd", [SEQ, 130], BF16).ap()),
        "Rd": Buf(nc.dram_tensor("Rd", [SEQ, 130], F32).ap()),
        "Rd2": Buf(nc.dram_tensor("Rd2", [SEQ, 130], F32).ap()),
        "load_h": load_h, "store_ob": store,
    }
    emit_W(nc, S, A, banks, io, npat)
    S.wait_dma("sp", [oT_d])
    return nc


def w_inputs(inp, j, hT_full, cst):
    w_in = inp["w_in"][0]
    base = 2064
    cols = np.concatenate([base + np.arange(j * 128, (j + 1) * 128), base + 512 + np.arange(j * 128, (j + 1) * 128),
                           base + 1024 + np.arange(j * 128, (j + 1) * 128)])
    w = np.ascontiguousarray(w_in[:, cols])
    nw = np.stack([np.tile(inp["q_norm_w"][0], 2), np.tile(inp["k_norm_w"][0], 2)], axis=1).astype(np.float32)
    bias_g, maskc = swa_tables(inp["rel_bias"], j)
    return {"hT": hT_full, "w": w, "nw": np.ascontiguousarray(nw), "bias_g": bias_g, "maskc": maskc, "cst": cst}


GROUPS = [[0, 1, 2, 3], [4, 5, 6, 7]]


def build_fused():
    nc = bass.Bass("TRN2", target_bir_lowering=False)
    S = Sched(nc)
    A = Alloc(nc)
    banks = make_banks(A)

    def ext(name, shape, dt=F32):
        return Buf(nc.dram_tensor(name, list(shape), dt, kind="ExternalInput").ap(), name)

    e = {
        "x": ext("x", [TOK, D]), "n1": ext("n1", [128, D]), "nm": ext("nm", [128, D]),
        "wg1": ext("wg1", [D, FF]), "wu1": ext("wu1", [D, FF]), "wd1": ext("wd1", [FF, D]),
        "ident": ext("ident", [128, 128]), "cst": ext("cst", [128, 8, 128]),
        "gw": ext("gw", [D, 516]), "cw": ext("cw", [128, 15]), "par": ext("par", [128, 4]), "gnw": ext("gnw", [128, 128]),
        "ww": ext("ww", [D, 384]), "nw": ext("nw", [128, 2]), "bias_g": ext("bias_g", [128, 12, 128]),
        "maskc": ext("maskc", [128, 4, 128]),
        "wo": ext("wo", [D, D]), "n2": ext("n2", [128, D]), "nf": ext("nf", [128, D]),
        "wg2": ext("wg2", [D, FF]), "wu2": ext("wu2", [D, FF]), "wd2": ext("wd2", [FF, D]),
    }
    out_d = Buf(nc.dram_tensor("out", [TOK, D], F32, kind="ExternalOutput").ap(), "out")
    out_d.persist = True
    x1_d = Buf(nc.dram_tensor("x1s", [TOK, D], F32).ap(), "x1s")
    x1_d.persist = True
    hb_t = [nc.dram_tensor(f"hbnc{i}", [1024, 256], F32) for i in range(4)]
    hbB = [Buf(t.ap(), f"hbnc{i}") for i, t in enumerate(hb_t)]
    HG = nc.dram_tensor("hgath", [4, 4096, 256], F32)
    hgB = [Buf(HG.ap()[i], f"hg{i}") for i in range(4)]
    oa_t = [nc.dram_tensor(f"oanc{i}", [128, 1024], F32) for i in range(4)]
    oaB = [Buf(t.ap(), f"oanc{i}") for i, t in enumerate(oa_t)]
    ob_t = [nc.dram_tensor(f"obnc{i}", [128, 1024], F32) for i in range(4)]
    obB = [Buf(t.ap(), f"obnc{i}") for i, t in enumerate(ob_t)]
    OGa = nc.dram_tensor("ogath_a", [4, 512, 1024], F32)
    OGb = nc.dram_tensor("ogath_b", [4, 512, 1024], F32)
    ogaB = Buf(OGa.ap(), "oga")
    ogbB = Buf(OGb.ap(), "ogb")
    for b in hbB + oaB + obB:
        b.persist = True
    cc_sem = nc.alloc_semaphore("cc_sem")
    cc = {"n": 0}

    def gather(src_t, srcB, dst_ap, dstB):
        S.wait_dma("pool", [srcB])
        ins = nc.gpsimd.collective_compute("AllGather", ALU.bypass, replica_groups=GROUPS,
                                           ins=[src_t.ap().opt()], outs=[dst_ap.opt()])
        ins.then_inc(cc_sem)
        cc["n"] += 1
        dstB.dsem = cc_sem
        dstB.dcnt = cc["n"]

    A.begin()

    def store_hT(C, tb):
        dv = hb_t[tb].ap().bitcast(BF16).rearrange("(c p) t -> p c t", p=128)
        S.dma("sp", hbB[tb], dv, C.nTb[tb], C.nT[:, :, tb * 512:(tb + 1) * 512])
        gather(hb_t[tb], hbB[tb], HG.ap()[tb], hgB[tb])

    emit_A(nc, S, A, banks, {"x_d": e["x"], "n1_d": e["n1"], "nm_d": e["nm"], "wg_d": e["wg1"], "wu_d": e["wu1"],
                             "wd_d": e["wd1"], "id_d": e["ident"], "x1_d": x1_d, "store_hT": store_hT})
    S.end_phase()
    A.end()

    def load_h(b, hb):
        r, bb = b // 4, b % 4
        src = HG.ap()[bb].bitcast(BF16).rearrange("(r c p) t -> r p c t", r=4, c=KC)[r]
        S.dma("sp", hb, hb[:], hgB[bb], src)

    def make_store(bnc_t, bncB, OGx, ogxB, do_gather=True):
        def store(oT, oTb, quarter=None):
            for q in (range(4) if quarter is None else [quarter]):
                dv = bnc_t[q].ap().bitcast(BF16)
                for bb in range(4):
                    b = 4 * q + bb
                    S.dma("sp", bncB[q], dv[:, bb * 512:(bb + 1) * 512], oTb[b], oT[:, b * 512:(b + 1) * 512])
                if do_gather:
                    gather(bnc_t[q], bncB[q], OGx.ap()[q], ogxB)
        return store

    def oa_gathers():
        for q in range(4):
            gather(oa_t[q], oaB[q], OGa.ap()[q], ogaB)

    A.begin()
    emit_G(nc, S, A, banks, {"gw_d": e["gw"], "cw_d": e["cw"], "par_d": e["par"], "gnw_d": e["gnw"], "cst_d": e["cst"],
                             "load_h": load_h, "store_oa": make_store(oa_t, oaB, OGa, ogaB, do_gather=False)})
    S.end_phase()
    A.end()
    A.begin()
    Vd = Buf(nc.dram_tensor("Vd", [SEQ, 130], BF16).ap(), "Vd")
    Rd = Buf(nc.dram_tensor("Rd", [SEQ, 130], F32).ap(), "Rd")
    Rd2 = Buf(nc.dram_tensor("Rd2", [SEQ, 130], F32).ap(), "Rd2")
    emit_W(nc, S, A, banks, {"ww_d": e["ww"], "nw_d": e["nw"], "bg_d": e["bias_g"], "mk_d": e["maskc"], "cst_d": e["cst"],
                             "Vd": Vd, "Rd": Rd, "Rd2": Rd2, "after_setup": oa_gathers, "load_h": load_h, "store_ob": make_store(ob_t, obB, OGb, ogbB)})
    S.end_phase()
    A.end()

    A.begin()

    def load_oT(C):
        qv = nc.sync.partition_id() % 4
        for sidx, (OGx, ogxB) in enumerate(((OGa, ogaB), (OGb, ogbB))):
            src = OGx.ap().bitcast(BF16)[qv].rearrange("(r p) t -> p r t", p=128)
            for tb in range(4):
                S.dma("sp", C.nTb[tb], C.nT[:, sidx::2, tb * 512:(tb + 1) * 512], ogxB, src[:, :, tb * 512:(tb + 1) * 512])

    emit_C(nc, S, A, banks, {"x_d": x1_d, "wo_d": e["wo"], "n2_d": e["n2"], "nf_d": e["nf"], "wg_d": e["wg2"],
                             "wu_d": e["wu2"], "wd_d": e["wd2"], "id_d": e["ident"], "out_d": out_d,
                             "wo_rows": lambda c: (c // 2) * 128 + (c % 2) * 512, "load_oT": load_oT})
    S.wait_dma("sp", [out_d])
    A.end()
    return nc


def _rep(v):
    return np.ascontiguousarray(np.broadcast_to(np.asarray(v, np.float32).reshape(1, -1), (128, v.size)))


def kernel(**inp):
    inp = {k: np.asarray(v) for k, v in inp.items()}
    cores = list(range(8))
    x = np.ascontiguousarray(inp["x"], dtype=np.float32).reshape(8, TOK, D)
    ident = np.eye(128, dtype=np.float32)
    cst = host_consts()
    shared = {
        "n1": _rep(inp["ffn1_norm"][0]), "nm": _rep(inp["mix_norm"][0]),
        "wg1": inp["ffn1_w_gate"][0], "wu1": inp["ffn1_w_up"][0], "wd1": inp["ffn1_w_down"][0],
        "ident": ident, "cst": cst,
        "wo": inp["w_out"][0], "n2": _rep(inp["ffn2_norm"][0]), "nf": _rep(inp["final_norm"][0]),
        "wg2": inp["ffn2_w_gate"][0], "wu2": inp["ffn2_w_up"][0], "wd2": inp["ffn2_w_down"][0],
    }
    maps = []
    for c in cores:
        j = c % 4
        g = g_inputs(inp, j, None, cst)
        w = w_inputs(inp, j, None, cst)
        m = dict(shared)
        m.update({"x": x[c], "gw": g["w"], "cw": g["cw"], "par": g["par"], "gnw": g["gw"],
                  "ww": w["w"], "nw": w["nw"], "bias_g": w["bias_g"], "maskc": w["maskc"]})
        maps.append({k: np.ascontiguousarray(v, dtype=np.float32) for k, v in m.items()})
    nc = build_fused()
    res = run_bass_kernel_spmd(nc, maps, core_ids=cores).results
    return np.stack([res[c]["out"] for c in cores]).reshape(2, SEQ, D).astype(np.float32)
```

```python
import contextlib
import numpy as np
import ml_dtypes
import concourse.bass as bass
import concourse.mybir as mybir
from concourse.bass_utils import run_bass_kernel_spmd

F32 = mybir.dt.float32
BF16 = mybir.dt.bfloat16
AF = mybir.ActivationFunctionType
ALU = mybir.AluOpType

D = 1024
KC = 8
FF = 2816
NF = 22
SEQ = 8192
TOK = 2048
NT = TOK // 128
EPS = 1e-6
PARTS = (6, 6, 5, 5)


class Buf:
    def __init__(self, ap, name="", excl=None):
        self.ap = ap
        self.name = name
        self.excl = self if excl == "self" else excl
        self.w = None
        self.r = {}
        self.dsem = None
        self.dcnt = 0
        self.rsem = None
        self.rcnt = 0

    def __getitem__(self, idx):
        return self.ap[idx]


class Sched:
    ENG = ("pe", "act", "dve", "pool", "sp")

    def __init__(self, nc):
        self.nc = nc
        self.e = {"pe": nc.tensor, "act": nc.scalar, "dve": nc.vector, "pool": nc.gpsimd, "sp": nc.sync}
        self.sem = {k: nc.alloc_semaphore("s_" + k) for k in self.ENG}
        self.cnt = {k: 0 for k in self.ENG}
        self.seen = {k: {} for k in self.ENG}
        self.same_engine_sync = True
        self.nsem = 0
        self.dsems = []
        self.sempool = {}

    def newsem(self, name):
        self.nsem += 1
        return self.nc.alloc_semaphore(f"{name}_{self.nsem}")

    def _wait(self, eng, sem, key, val):
        if val <= 0 or self.seen[eng].get(key, 0) >= val:
            return
        self.e[eng].wait_ge(sem, val)
        self.seen[eng][key] = val

    def _deps(self, eng, reads, writes):
        for b in reads:
            if b.w is not None:
                we, wc = b.w
                if we != eng or (self.same_engine_sync and eng != "pe"):
                    self._wait(eng, self.sem[we], we, wc)
            if b.dsem is not None and b.dcnt:
                self._wait(eng, b.dsem, id(b.dsem), b.dcnt)
        for b in writes:
            if b.w is not None:
                we, wc = b.w
                if we != eng or (self.same_engine_sync and eng != "pe"):
                    self._wait(eng, self.sem[we], we, wc)
            for re_, rc in b.r.items():
                if re_ != eng:
                    self._wait(eng, self.sem[re_], re_, rc)
            if b.dsem is not None and b.dcnt:
                self._wait(eng, b.dsem, id(b.dsem), b.dcnt)
            if b.rsem is not None and b.rcnt:
                self._wait(eng, b.rsem, id(b.rsem), b.rcnt)

    def op(self, eng, fn, reads=(), writes=(), acc=False):
        ex = []
        for b in list(reads) + list(writes):
            if b.excl is not None and b.excl not in ex:
                ex.append(b.excl)
        reads = [b for b in reads if b.excl is None]
        writes = [b for b in writes if b.excl is None] + ex
        self._deps(eng, reads, () if acc else writes)
        ins = fn(self.e[eng])
        self.cnt[eng] += 1
        ins.then_inc(self.sem[eng], 1)
        c = self.cnt[eng]
        for b in reads:
            b.r[eng] = c
        for b in writes:
            b.w = (eng, c)
            if not acc:
                b.r = {}
        return ins

    def dma(self, eng, out_buf, out_ap, in_buf, in_ap, **kw):
        self._deps(eng, [in_buf], [out_buf])
        if out_buf.dsem is None:
            pool = self.sempool.setdefault(eng, [])
            if pool:
                out_buf.dsem, out_buf.dcnt = pool.pop()
            else:
                out_buf.dsem = self.newsem("dw")
            out_buf.dq = eng
            self.dsems.append(out_buf)
        assert getattr(out_buf, "dq", eng) == eng, f"buffer {out_buf.name} written by DMAs from two queues"
        ins = self.e[eng].dma_start(out=out_ap, in_=in_ap, **kw)
        ins.then_inc(out_buf.dsem, 16)
        out_buf.dcnt += 16
        in_buf.rsem = out_buf.dsem
        in_buf.rcnt = out_buf.dcnt
        return ins

    def wait_dma(self, eng, bufs):
        for b in bufs:
            if b.dsem is not None:
                self._wait(eng, b.dsem, id(b.dsem), b.dcnt)

    def end_phase(self):
        self.barrier()
        keep = []
        for b in self.dsems:
            if getattr(b, "persist", False):
                keep.append(b)
            else:
                self.sempool.setdefault(b.dq, []).append((b.dsem, b.dcnt))
        self.dsems = keep

    def barrier(self):
        for eng in self.ENG:
            for o in self.ENG:
                if o != eng:
                    self._wait(eng, self.sem[o], o, self.cnt[o])
            for b in self.dsems:
                self._wait(eng, b.dsem, id(b.dsem), b.dcnt)


class Alloc:
    def __init__(self, nc):
        self.nc = nc
        self.n = 0
        self.stack = None

    def begin(self):
        self.stack = contextlib.ExitStack()

    def end(self):
        self.stack.close()
        self.stack = None

    def sb(self, shape, dt, name=None):
        self.n += 1
        nm = f"{name or 't'}_{self.n}"
        if self.stack is None:
            return self.nc.alloc_sbuf_tensor(nm, list(shape), dt).ap()
        t = self.stack.enter_context(self.nc.sbuf_tensor(nm, list(shape), dt))
        return t.ap()

    def ps(self, shape, dt=F32, name=None):
        self.n += 1
        return self.nc.alloc_psum_tensor(f"{name or 'p'}_{self.n}", list(shape), dt).ap()


def make_banks(A):
    return [Buf(A.ps([128, 512]), f"bank{i}", excl="self") for i in range(8)]


class FfnCtx:
    def __init__(self, nc, S, A, banks):
        self.nc, self.S, self.A = nc, S, A
        self.X = A.sb([128, NT, D], F32, "X")
        self.Xb = [Buf(self.X[:, t, :], f"X{t}") for t in range(NT)]
        self.nT = A.sb([128, KC, TOK], BF16, "nT")
        self.nTb = [Buf(self.nT[:, :, tb * 512:(tb + 1) * 512], f"nT{tb}") for tb in range(4)]
        nfp = max(PARTS)
        self.act = A.sb([128, nfp, TOK], BF16, "act")
        self.actb = [Buf(self.act[:, :, tb * 512:(tb + 1) * 512], f"act{tb}") for tb in range(4)]
        self.wd = [A.sb([128, nfp, D], BF16, "wd") for _ in range(2)]
        self.wdb = [Buf(w, "wd") for w in self.wd]
        self.wg = [A.sb([128, KC, 256], BF16, "wg") for _ in range(2)]
        self.wu = [A.sb([128, KC, 256], BF16, "wu") for _ in range(2)]
        self.wgb = [Buf(w, "wg") for w in self.wg]
        self.wub = [Buf(w, "wu") for w in self.wu]
        self.wrep = [Buf(A.sb([128, D], F32, "wrep"), "wrep") for _ in range(2)]
        self.nb = [Buf(A.sb([128, D], BF16, "nb"), "nb") for _ in range(4)]
        self.junk = Buf(A.sb([128, D], F32, "junk"), "junk")
        self.ssall = Buf(A.sb([128, NT], F32, "ssall"), "ssall")
        self.sg = [Buf(A.sb([128, 512], F32, "sg"), "sg") for _ in range(2)]
        self.ident = Buf(A.sb([128, 128], BF16, "ident"), "ident")
        self.PG = banks[0:2]
        self.PU = banks[2:4]
        self.PD = banks[4:6]
        self.PT = [Buf(banks[4 + i].ap.bitcast(BF16)[:, 0:KC * 128].rearrange("p (c t) -> p c t", c=KC), "pt",
                       excl=banks[4 + i]) for i in range(4)]
        self.i_n = 0
        self.i_g = 0
        self.i_d = 0
        self.i_w = 0
        self.i_wd = 0


def emit_rstd(C):
    S = C.S
    for tt in range(NT):
        xb = C.Xb[tt]
        S.op("act", lambda e: e.activation(out=C.junk[:], in_=xb[:], func=AF.Square, accum_out=C.ssall[:, tt:tt + 1]),
             [xb], [C.junk, C.ssall])
    S.op("act", lambda e: e.activation(out=C.ssall[:], in_=C.ssall[:], func=AF.Sqrt, scale=1.0 / D, bias=EPS),
         [C.ssall], [C.ssall])
    S.op("dve", lambda e: e.reciprocal(out=C.ssall[:], in_=C.ssall[:]), [C.ssall], [C.ssall])


def emit_norm_T(C, wrep, dst_write):
    S = C.S
    emit_rstd(C)
    for tt in range(NT):
        i = C.i_n
        C.i_n += 1
        nb, pt = C.nb[i % 4], C.PT[i % 4]
        xb = C.Xb[tt]
        S.op("dve", lambda e: e.scalar_tensor_tensor(out=nb[:], in0=xb[:], scalar=C.ssall[:, tt:tt + 1], in1=wrep[:],
                                                     op0=ALU.mult, op1=ALU.mult), [xb, C.ssall, wrep], [nb])
        for c in range(KC):
            S.op("pe", lambda e: e.transpose(pt[:, c, :], nb[:, c * 128:(c + 1) * 128], C.ident[:]),
                 [nb, C.ident], [pt], acc=(c > 0))
        dst_write(tt, pt, "act" if tt % 2 == 0 else "dve")


def emit_ffn(C, wrep, wg_d, wu_d, wd_d):
    S = C.S

    def evac(tt, pt, eng):
        tb = tt // 4
        if eng == "act":
            S.op("act", lambda e: e.copy(out=C.nT[:, :, tt * 128:(tt + 1) * 128], in_=pt[:]), [pt], [C.nTb[tb]])
        else:
            S.op("dve", lambda e: e.tensor_copy(out=C.nT[:, :, tt * 128:(tt + 1) * 128], in_=pt[:]), [pt], [C.nTb[tb]])

    emit_norm_T(C, wrep, evac)
    wg_v = wg_d.ap.rearrange("(c p) n -> p c n", p=128)
    wu_v = wu_d.ap.rearrange("(c p) n -> p c n", p=128)
    wd_v = wd_d.ap.rearrange("(f p) n -> p f n", p=128)
    f0 = 0
    for pi, nfp in enumerate(PARTS):
        wdi = C.i_wd % 2
        C.i_wd += 1
        S.dma("pool", C.wdb[wdi], C.wd[wdi][:, 0:nfp, :], wd_d, wd_v[:, f0:f0 + nfp, :])
        fl = 0
        while fl < nfp:
            gs = min(2, nfp - fl)
            wi = C.i_w % 2
            C.i_w += 1
            cs = (f0 + fl) * 128
            S.dma("pool", C.wgb[wi], C.wg[wi][:, :, 0:gs * 128], wg_d, wg_v[:, :, cs:cs + gs * 128])
            S.dma("pool", C.wub[wi], C.wu[wi][:, :, 0:gs * 128], wu_d, wu_v[:, :, cs:cs + gs * 128])
            for g in range(gs):
                for tb in range(4):
                    i = C.i_g
                    C.i_g += 1
                    pg, pu, sg = C.PG[i % 2], C.PU[i % 2], C.sg[i % 2]
                    for c in range(KC):
                        S.op("pe", lambda e: e.matmul(pg[:], lhsT=C.wg[wi][:, c, g * 128:(g + 1) * 128],
                                                      rhs=C.nT[:, c, tb * 512:(tb + 1) * 512],
                                                      start=(c == 0), stop=(c == KC - 1)),
                             [C.wgb[wi], C.nTb[tb]], [pg], acc=(c > 0))
                    for c in range(KC):
                        S.op("pe", lambda e: e.matmul(pu[:], lhsT=C.wu[wi][:, c, g * 128:(g + 1) * 128],
                                                      rhs=C.nT[:, c, tb * 512:(tb + 1) * 512],
                                                      start=(c == 0), stop=(c == KC - 1)),
                             [C.wub[wi], C.nTb[tb]], [pu], acc=(c > 0))
                    S.op("act", lambda e: e.activation(out=sg[:], in_=pg[:], func=AF.Silu), [pg], [sg])
                    S.op("dve", lambda e: e.tensor_tensor(out=C.act[:, fl + g, tb * 512:(tb + 1) * 512],
                                                          in0=pu[:], in1=sg[:], op=ALU.mult),
                         [pu, sg], [C.actb[tb]])
            fl += gs
        for tt in range(NT):
            tb = tt // 4
            for dh in range(2):
                j = C.i_d
                C.i_d += 1
                pd = C.PD[j % 2]
                for f in range(nfp):
                    S.op("pe", lambda e: e.matmul(pd[:], lhsT=C.act[:, f, tt * 128:(tt + 1) * 128],
                                                  rhs=C.wd[wdi][:, f, dh * 512:(dh + 1) * 512],
                                                  start=(f == 0), stop=(f == nfp - 1)),
                         [C.actb[tb], C.wdb[wdi]], [pd], acc=(f > 0))
                xs = C.X[:, tt, dh * 512:(dh + 1) * 512]
                S.op("dve", lambda e: e.scalar_tensor_tensor(out=xs, in0=pd[:], scalar=0.5, in1=xs,
                                                             op0=ALU.mult, op1=ALU.add),
                     [pd, C.Xb[tt]], [C.Xb[tt]])
        f0 += nfp


def load_X(C, x_d):
    xv = x_d.ap.rearrange("(t p) d -> p t d", p=128)
    for t in range(NT):
        C.S.dma("sp", C.Xb[t], C.X[:, t, :], x_d, xv[:, t, :])


def emit_A(nc, S, A, banks, io):
    C = FfnCtx(nc, S, A, banks)
    S.dma("pool", C.ident, C.ident[:], io["id_d"], io["id_d"][:, :])
    S.dma("sp", C.wrep[0], C.wrep[0][:], io["n1_d"], io["n1_d"][:, :])
    S.dma("sp", C.wrep[1], C.wrep[1][:], io["nm_d"], io["nm_d"][:, :])
    load_X(C, io["x_d"])
    emit_ffn(C, C.wrep[0], io["wg_d"], io["wu_d"], io["wd_d"])
    x1_d = io["x1_d"]
    x1v = x1_d.ap.rearrange("(t p) d -> p t d", p=128)
    for t in range(NT):
        S.dma("sp", x1_d, x1v[:, t, :], C.Xb[t], C.X[:, t, :])

    def evac(tt, pt, eng):
        tb = tt // 4
        if eng == "act":
            S.op("act", lambda e: e.copy(out=C.nT[:, :, tt * 128:(tt + 1) * 128], in_=pt[:]), [pt], [C.nTb[tb]])
        else:
            S.op("dve", lambda e: e.tensor_copy(out=C.nT[:, :, tt * 128:(tt + 1) * 128], in_=pt[:]), [pt], [C.nTb[tb]])
        if tt % 4 == 3:
            io["store_hT"](C, tb)

    emit_norm_T(C, C.wrep[1], evac)
    return C


def build_A():
    nc = bass.Bass("TRN2", target_bir_lowering=False)
    S = Sched(nc)
    A = Alloc(nc)
    banks = make_banks(A)
    io = {
        "x_d": Buf(nc.dram_tensor("x", [TOK, D], F32, kind="ExternalInput").ap(), "x"),
        "n1_d": Buf(nc.dram_tensor("n1", [128, D], F32, kind="ExternalInput").ap()),
        "nm_d": Buf(nc.dram_tensor("nm", [128, D], F32, kind="ExternalInput").ap()),
        "wg_d": Buf(nc.dram_tensor("wg", [D, FF], F32, kind="ExternalInput").ap()),
        "wu_d": Buf(nc.dram_tensor("wu", [D, FF], F32, kind="ExternalInput").ap()),
        "wd_d": Buf(nc.dram_tensor("wd", [FF, D], F32, kind="ExternalInput").ap()),
        "id_d": Buf(nc.dram_tensor("ident", [128, 128], F32, kind="ExternalInput").ap()),
        "x1_d": Buf(nc.dram_tensor("x1", [TOK, D], F32, kind="ExternalOutput").ap()),
    }
    hT_d = Buf(nc.dram_tensor("hT", [D, TOK], BF16, kind="ExternalOutput").ap())

    def store_hT(C, tb):
        hv = hT_d.ap.rearrange("(c p) t -> p c t", p=128)
        S.dma("sp", hT_d, hv[:, :, tb * 512:(tb + 1) * 512], C.nTb[tb], C.nT[:, :, tb * 512:(tb + 1) * 512])

    io["store_hT"] = store_hT
    emit_A(nc, S, A, banks, io)
    S.wait_dma("sp", [io["x1_d"], hT_d])
    return nc


def emit_C(nc, S, A, banks, io):
    C = FfnCtx(nc, S, A, banks)
    S.dma("pool", C.ident, C.ident[:], io["id_d"], io["id_d"][:, :])
    S.dma("sp", C.wrep[0], C.wrep[0][:], io["n2_d"], io["n2_d"][:, :])
    S.dma("sp", C.wrep[1], C.wrep[1][:], io["nf_d"], io["nf_d"][:, :])
    load_X(C, io["x_d"])
    io["load_oT"](C)
    wo_d = io["wo_d"]
    for c in range(KC):
        r0 = io["wo_rows"](c)
        S.dma("pool", C.wdb[c // 4], C.wd[c // 4][:, c % 4, :], wo_d, wo_d[r0:r0 + 128, :])
    for tt in range(NT):
        tb = tt // 4
        for dh in range(2):
            j = C.i_d
            C.i_d += 1
            pd = C.PD[j % 2]
            for c in range(KC):
                S.op("pe", lambda e: e.matmul(pd[:], lhsT=C.nT[:, c, tt * 128:(tt + 1) * 128],
                                              rhs=C.wd[c // 4][:, c % 4, dh * 512:(dh + 1) * 512],
                                              start=(c == 0), stop=(c == KC - 1)),
                     [C.nTb[tb], C.wdb[c // 4]], [pd], acc=(c > 0))
            xs = C.X[:, tt, dh * 512:(dh + 1) * 512]
            S.op("dve", lambda e: e.tensor_tensor(out=xs, in0=pd[:], in1=xs, op=ALU.add),
                 [pd, C.Xb[tt]], [C.Xb[tt]])
    emit_ffn(C, C.wrep[0], io["wg_d"], io["wu_d"], io["wd_d"])
    out_d = io["out_d"]
    ov2 = out_d.ap.rearrange("(t p) d -> p t d", p=128)
    emit_rstd(C)
    for tt in range(NT):
        xb = C.Xb[tt]
        S.op("dve", lambda e: e.scalar_tensor_tensor(out=xb[:], in0=xb[:], scalar=C.ssall[:, tt:tt + 1], in1=C.wrep[1][:],
                                                   op0=ALU.mult, op1=ALU.mult), [xb, C.ssall, C.wrep[1]], [xb])
        S.dma("sp", out_d, ov2[:, tt, :], xb, C.X[:, tt, :])
    return C


def build_C():
    nc = bass.Bass("TRN2", target_bir_lowering=False)
    S = Sched(nc)
    A = Alloc(nc)
    banks = make_banks(A)
    oT_d = Buf(nc.dram_tensor("oT", [D, TOK], BF16, kind="ExternalInput").ap())
    io = {
        "x_d": Buf(nc.dram_tensor("x1", [TOK, D], F32, kind="ExternalInput").ap(), "x1"),
        "wo_d": Buf(nc.dram_tensor("wo", [D, D], F32, kind="ExternalInput").ap()),
        "n2_d": Buf(nc.dram_tensor("n2", [128, D], F32, kind="ExternalInput").ap()),
        "nf_d": Buf(nc.dram_tensor("nf", [128, D], F32, kind="ExternalInput").ap()),
        "wg_d": Buf(nc.dram_tensor("wg", [D, FF], F32, kind="ExternalInput").ap()),
        "wu_d": Buf(nc.dram_tensor("wu", [D, FF], F32, kind="ExternalInput").ap()),
        "wd_d": Buf(nc.dram_tensor("wd", [FF, D], F32, kind="ExternalInput").ap()),
        "id_d": Buf(nc.dram_tensor("ident", [128, 128], F32, kind="ExternalInput").ap()),
        "out_d": Buf(nc.dram_tensor("out", [TOK, D], F32, kind="ExternalOutput").ap()),
        "wo_rows": lambda c: c * 128,
    }

    def load_oT(C):
        ov = oT_d.ap.rearrange("(c p) t -> p c t", p=128)
        for tb in range(4):
            S.dma("sp", C.nTb[tb], C.nT[:, :, tb * 512:(tb + 1) * 512], oT_d, ov[:, :, tb * 512:(tb + 1) * 512])

    io["load_oT"] = load_oT
    emit_C(nc, S, A, banks, io)
    S.wait_dma("sp", [io["out_d"]])
    return nc


class Ring:
    def __init__(self, bufs):
        self.bufs = bufs
        self.i = 0

    def get(self):
        b = self.bufs[self.i % len(self.bufs)]
        self.i += 1
        return b


NB = SEQ // 512
NCH = SEQ // 128
BIG = 1.0e5
C_ID, C_ONES, C_TRIU, C_TRIL, C_MF, C_MB, C_OFFD, C_BLK = range(8)


def host_consts():
    i = np.arange(128)[:, None]
    j = np.arange(128)[None, :]
    c = np.zeros((8, 128, 128), np.float32)
    c[C_ID] = (i == j)
    c[C_ONES] = 1.0
    c[C_TRIU] = (i <= j)
    c[C_TRIL] = (i >= j)
    c[C_MF] = np.where(j <= i, 0.0, BIG)
    c[C_MB] = np.where(j >= i, 0.0, BIG)
    c[C_OFFD] = (i != j)
    c[C_BLK] = ((i // 64) == (j // 64))
    return np.ascontiguousarray(c.transpose(1, 0, 2))


def emit_G(nc, S, A, banks, io, nsteps=NCH, plimit=9):
    w_d, cw_d, par_d, gw_d, cst_d = io["gw_d"], io["cw_d"], io["par_d"], io["gnw_d"], io["cst_d"]

    cst = Buf(A.sb([128, 8, 128], F32, "cst"))
    cstb = Buf(A.sb([128, 8, 128], BF16, "cstb"))
    R4 = A.sb([128, KC * 516], BF16, "R4")
    W = Buf(R4.rearrange("p (c n) -> p c n", c=KC))
    cw = Buf(A.sb([128, 15], F32, "cw"))
    par = Buf(A.sb([128, 4], F32, "par"))
    gw = Buf(A.sb([128, 128], F32, "gw"))
    dg = Buf(A.sb([128, 15, 128], BF16, "dg"))
    QaT = A.sb([128, SEQ], BF16, "QaT")
    KaT = A.sb([128, SEQ], BF16, "KaT")
    VaT = A.sb([128, SEQ], BF16, "VaT")
    QaTb = [Buf(QaT[:, b * 512:(b + 1) * 512]) for b in range(NB)]
    KaTb = [Buf(KaT[:, b * 512:(b + 1) * 512]) for b in range(NB)]
    VaTb = [Buf(VaT[:, b * 512:(b + 1) * 512]) for b in range(NB)]
    Ktok = A.sb([128, NCH, 128], BF16, "Ktok")
    Vtok = A.sb([128, NCH, 128], BF16, "Vtok")
    Ktokb = None
    Vtokb = None
    sz = A.sb([128, NCH, 128], BF16, "sz")
    szb = [Buf(sz[:, c, :]) for c in range(NCH)]
    ab = Buf(A.sb([128, 4, NCH], F32, "ab"))
    R1 = A.sb([128, 3 * 8200], BF16, "R1")
    pre = [R1[:, m * 8200:m * 8200 + SEQ + 4] for m in range(3)]
    oacc = R1.bitcast(F32)[:, 0:SEQ].rearrange("p (c d) -> p c d", d=128)
    oaccb = [Buf(oacc[:, c, :]) for c in range(NCH)]
    preb = [[Buf(pre[m][:, 2 + b * 512: 2 + (b + 1) * 512]) for b in range(NB)] for m in range(3)]
    prepad = [Buf(pre[m][:, 0:2]) for m in range(3)] + [Buf(pre[m][:, SEQ + 2:SEQ + 4]) for m in range(3)]
    R2 = A.sb([128, 2 * KC * 512], BF16, "R2")
    hblk = [Buf(R2[:, i * KC * 512:(i + 1) * KC * 512].rearrange("p (c t) -> p c t", c=KC)) for i in range(2)]

    def cf(k):
        return cst[:, k, :]

    def cb(k):
        return cstb[:, k, :]

    S.dma("sp", cst, cst[:], cst_d, cst_d[:, :, :])
    S.dma("pool", cstb, cstb[:], cst_d, cst_d[:, :, :])
    S.dma("pool", W, W[:], w_d, w_d.ap.rearrange("(c p) n -> p c n", p=128))
    S.dma("sp", cw, cw[:], cw_d, cw_d[:, :])
    S.dma("sp", par, par[:], par_d, par_d[:, :])
    S.dma("sp", gw, gw[:], gw_d, gw_d[:, :])
    for m in range(3):
        S.op("pool", lambda e: e.memset(pre[m][:, 0:2], 0.0), [], [prepad[m]])
        S.op("pool", lambda e: e.memset(pre[m][:, SEQ + 2:SEQ + 4], 0.0), [], [prepad[3 + m]])
    for k in range(15):
        S.op("dve", lambda e: e.tensor_scalar(out=dg[:, k, :], in0=cf(C_ID), scalar1=cw[:, k:k + 1], scalar2=None,
                                              op0=ALU.mult), [cst, cw], [dg])

    pbr = Ring(banks[0:8])
    psr = Ring([Buf(banks[4 + (i % 4)][:, (i // 4) * 128:(i // 4 + 1) * 128], excl=banks[4 + (i % 4)]) for i in range(16)])

    if plimit <= 0:
        S.barrier()
        return
    R3 = A.sb([128, 4096], BF16, "R3")
    R3f = R3.bitcast(F32)
    tmp512 = Ring([Buf(R3f[:, i * 512:(i + 1) * 512]) for i in range(4)])
    for bi, b in enumerate([r_ * 4 + bb_ for bb_ in range(4) for r_ in range(4)]):
        hb = hblk[bi % 2]
        io["load_h"](b, hb)
        for m in range(3):
            p = pbr.get()
            for c in range(KC):
                S.op("pe", lambda e: e.matmul(p[:], lhsT=W[:, c, m * 128:(m + 1) * 128], rhs=hb[:, c, :],
                                              start=(c == 0), stop=(c == KC - 1)), [W, hb], [p], acc=(c > 0))
            S.op("act", lambda e: e.copy(out=pre[m][:, 2 + b * 512:2 + (b + 1) * 512], in_=p[:]), [p], [preb[m][b]])
        for t in range(4):
            ch = b * 4 + t
            p = pbr.get()
            for c in range(KC):
                S.op("pe", lambda e: e.matmul(p[:, 0:132], lhsT=hb[:, c, t * 128:(t + 1) * 128], rhs=W[:, c, 384:516],
                                              start=(c == 0), stop=(c == KC - 1)), [W, hb], [p], acc=(c > 0))
            S.op("act", lambda e: e.activation(out=sz[:, ch, :], in_=p[:, 0:128], func=AF.Silu), [p], [szb[ch]])
            S.op("dve", lambda e: e.tensor_copy(out=ab[:, :, ch], in_=p[:, 128:132]), [p], [ab])

    if plimit <= 1:
        S.barrier()
        return
    S.barrier()
    R2f_ = R2.bitcast(F32)
    NSET = 3
    sT = [Buf(R2f_[:, i * 512:(i + 1) * 512]) for i in range(6)]
    rT = [Buf(R3f[:, i * 512:(i + 1) * 512]) for i in range(4)]
    qT_ = [Buf(R4[:, i * 512:(i + 1) * 512]) for i in range(6)]
    dk_scale = 128.0
    trb = Ring([Buf(banks[7].ap.bitcast(BF16)[:, i * 256:i * 256 + 128], excl=banks[7]) for i in range(4)])

    def conv_main(b):
        nbrs = [bb for bb in (b - 1, b, b + 1) if 0 <= bb < NB]
        for m in range(3):
            p = banks[3 * (b % 2) + m]
            rd = [preb[m][bb] for bb in nbrs] + [dg] + ([prepad[m]] if b == 0 else []) + ([prepad[3 + m]] if b == NB - 1 else [])
            for i in range(5):
                S.op("pe", lambda e: e.matmul(p[:], lhsT=dg[:, m * 5 + i, :], rhs=pre[m][:, b * 512 + i: b * 512 + i + 512],
                                              start=(i == 0), stop=(i == 4)), rd, [p], acc=(i > 0))
            if m == 2:
                S.op("act", lambda e: e.activation(out=VaT[:, b * 512:(b + 1) * 512], in_=p[:], func=AF.Silu), [p], [VaTb[b]])
            else:
                s_, q_ = sT[2 * (b % NSET) + m], qT_[2 * (b % NSET) + m]
                S.op("act", lambda e: e.activation(out=s_[:], in_=p[:], func=AF.Silu), [p], [s_])
                S.op("pool", lambda e: e.tensor_tensor(out=q_[:], in0=s_[:], in1=s_[:], op=ALU.mult), [s_], [q_])

    def conv_chain(b):
        for m in range(2):
            s_, q_, r_ = sT[2 * (b % NSET) + m], qT_[2 * (b % NSET) + m], rT[2 * (b % 2) + m]
            p2 = banks[6]
            S.op("pe", lambda e: e.matmul(p2[:], lhsT=cb(C_ONES), rhs=q_[:], start=True, stop=True), [cstb, q_], [p2])
            sc = dk_scale if m == 0 else 1.0
            S.op("act", lambda e: e.activation(out=r_[:], in_=p2[:], func=AF.Sqrt, scale=sc, bias=EPS * sc), [p2], [r_])
            S.op("dve", lambda e: e.reciprocal(out=r_[:], in_=r_[:]), [r_], [r_])
            dst, dstb = (QaT, QaTb) if m == 0 else (KaT, KaTb)
            S.op("dve", lambda e: e.tensor_tensor(out=dst[:, b * 512:(b + 1) * 512], in0=s_[:], in1=r_[:], op=ALU.mult),
                 [s_, r_], [dstb[b]])

    trK = Buf(banks[7].ap.bitcast(BF16)[:, 0:512].rearrange("p (t d) -> p t d", t=4), excl=banks[7])
    trV = Buf(banks[7].ap.bitcast(BF16)[:, 512:1024].rearrange("p (t d) -> p t d", t=4), excl=banks[7])
    KtokB = [Buf(Ktok[:, 4 * b_:4 * b_ + 4, :]) for b_ in range(NB)]
    VtokB = [Buf(Vtok[:, 4 * b_:4 * b_ + 4, :]) for b_ in range(NB)]

    def conv_trans(b):
        first = True
        for (src, srcb, trp) in ((KaT, KaTb, trK), (VaT, VaTb, trV)):
            for t in range(4):
                ch = b * 4 + t
                S.op("pe", lambda e: e.transpose(trp[:, t, :], src[:, ch * 128:(ch + 1) * 128], cb(C_ID)), [srcb[b], cstb], [trp],
                     acc=(not first))
                first = False
        S.op("act", lambda e: e.copy(out=Ktok[:, 4 * b:4 * b + 4, :], in_=trK[:]), [trK], [KtokB[b]])
        S.op("act", lambda e: e.copy(out=Vtok[:, 4 * b:4 * b + 4, :], in_=trV[:]), [trV], [VtokB[b]])

    Ktokb = [KtokB[c_ // 4] for c_ in range(NCH)]
    Vtokb = [VtokB[c_ // 4] for c_ in range(NCH)]
    LB, LC = 2, 3
    for i in range(NB + LC):
        if i < NB:
            conv_main(i)
        if 0 <= i - LB < NB:
            conv_chain(i - LB)
        if 0 <= i - LC < NB:
            conv_trans(i - LC)

    if plimit <= 2:
        S.barrier()
        return
    def small(name, shape=(128, NCH)):
        return Buf(A.sb(list(shape), F32, name))

    g = [small("g0"), small("g1")]
    beta = [small("b0"), small("b1")]
    G = [small("G0"), small("G1")]
    eG = [small("eG0"), small("eG1")]
    eGl = [small("eGl0"), small("eGl1")]
    kd = [small("kd0"), small("kd1")]
    bg = [small("bg0"), small("bg1")]
    nA = small("nA", (128, 2))
    t1, t2, t3 = small("t1"), small("t2"), small("t3")
    S.op("act", lambda e: e.activation(out=nA[:], in_=par[:, 0:2], func=AF.Exp), [par], [nA])
    S.op("dve", lambda e: e.tensor_scalar(out=nA[:], in0=nA[:], scalar1=-1.0, scalar2=None, op0=ALU.mult), [nA], [nA])
    for d in range(2):
        S.op("dve", lambda e: e.tensor_scalar(out=t1[:], in0=ab[:, d, :], scalar1=par[:, 2 + d:3 + d], scalar2=None, op0=ALU.add),
             [ab, par], [t1])
        S.op("dve", lambda e: e.tensor_scalar(out=t2[:], in0=t1[:], scalar1=-1.0, scalar2=None, op0=ALU.mult), [t1], [t2])
        S.op("dve", lambda e: e.tensor_tensor(out=t2[:], in0=t2[:], in1=t1[:], op=ALU.max), [t1, t2], [t2])
        S.op("act", lambda e: e.activation(out=t2[:], in_=t2[:], func=AF.Exp, scale=-1.0), [t2], [t2])
        S.op("act", lambda e: e.activation(out=t2[:], in_=t2[:], func=AF.Ln, bias=1.0), [t2], [t2])
        S.op("dve", lambda e: e.tensor_scalar(out=t1[:], in0=t1[:], scalar1=0.0, scalar2=None, op0=ALU.max), [t1], [t1])
        S.op("dve", lambda e: e.tensor_tensor(out=t1[:], in0=t1[:], in1=t2[:], op=ALU.add), [t1, t2], [t1])
        S.op("dve", lambda e: e.tensor_scalar(out=g[d][:], in0=t1[:], scalar1=nA[:, d:d + 1], scalar2=None, op0=ALU.mult),
             [t1, nA], [g[d]])
        S.op("act", lambda e: e.activation(out=beta[d][:], in_=ab[:, 2 + d, :], func=AF.Sigmoid), [ab], [beta[d]])
        p = psr.get()
        S.op("pe", lambda e: e.matmul(p[:, 0:NCH], lhsT=cf(C_TRIU if d == 0 else C_TRIL), rhs=g[d][:], start=True, stop=True),
             [cst, g[d]], [p])
        S.op("dve", lambda e: e.tensor_copy(out=G[d][:], in_=p[:, 0:NCH]), [p], [G[d]])
        p = psr.get()
        S.op("pe", lambda e: e.matmul(p[:, 0:NCH], lhsT=cf(C_ONES), rhs=g[d][:], start=True, stop=True), [cst, g[d]], [p])
        S.op("dve", lambda e: e.tensor_copy(out=t3[:], in_=p[:, 0:NCH]), [p], [t3])
        S.op("act", lambda e: e.activation(out=eG[d][:], in_=G[d][:], func=AF.Exp), [G[d]], [eG[d]])
        S.op("act", lambda e: e.activation(out=eGl[d][:], in_=t3[:], func=AF.Exp), [t3], [eGl[d]])
        S.op("dve", lambda e: e.tensor_tensor(out=t3[:], in0=t3[:], in1=G[d][:], op=ALU.subtract), [t3, G[d]], [t3])
        S.op("act", lambda e: e.activation(out=kd[d][:], in_=t3[:], func=AF.Exp), [t3], [kd[d]])
        S.op("dve", lambda e: e.tensor_tensor(out=bg[d][:], in0=beta[d][:], in1=eG[d][:], op=ALU.mult), [beta[d], eG[d]], [bg[d]])

    if plimit <= 3:
        S.barrier()
        return
    S.barrier()
    S.op("pool", lambda e: e.memset(oacc[:], 0.0), [], oaccb)
    Sst = [Buf(A.sb([128, 128], F32, "S")) for _ in range(2)]
    Sbf = [Buf(A.sb([128, 128], BF16, "Sb")) for _ in range(2)]
    for d in range(2):
        S.op("pool", lambda e: e.memset(Sst[d][:], 0.0), [], [Sst[d]])
        S.op("pool", lambda e: e.memset(Sbf[d][:], 0.0), [], [Sbf[d]])

    def mm(out, lhsT_ap, rhs_ap, rd):
        S.op("pe", lambda e: e.matmul(out[:], lhsT=lhsT_ap, rhs=rhs_ap, start=True, stop=True), rd, [out])

    def tr(out_bf_ap, out_buf, in_ap, rd):
        S.op("pe", lambda e: e.transpose(out_bf_ap, in_ap, cb(C_ID)), rd + [cstb], [out_buf])

    class Slot:
        pass

    NSLOT = 8
    regions = [(R2, 0), (R2, 4096), (R1, 16384), (R1, 16384 + 4096), (R3, 0), (R4, 0), (VaT, 0), (VaT, 4096)]
    slots = []
    for k in range(NSLOT):
        reg, base_bf = regions[k]
        regf = reg.bitcast(F32)
        fb = base_bf // 2

        def ft(i, n=1, _regf=regf, _fb=fb):
            return Buf(_regf[:, _fb + i * 128:_fb + (i + n) * 128])

        sl = Slot()
        sl.T = [ft(i) for i in range(4)]
        sl.PA, sl.PB, sl.PC = ft(4, 2), ft(6, 2), ft(8, 2)
        sl.RA, sl.RB = ft(10), ft(11)
        sl.UW = ft(12, 2)
        sl.bf = [Buf(reg[:, base_bf + 3584 + i * 128: base_bf + 3584 + (i + 1) * 128]) for i in range(4)]
        bk = banks[k]
        sl.bank = bk
        sl.p01 = Buf(bk[:, 0:256], excl=bk)
        sl.p2 = Buf(bk[:, 256:384], excl=bk)
        sl.p3 = Buf(bk[:, 384:512], excl=bk)
        sl.pq = [Buf(bk[:, i * 128:(i + 1) * 128], excl=bk) for i in range(4)]
        slots.append(sl)
    chain_done = {0: -1, 1: -1}

    def job(s, d, R):
        c = s if d == 0 else NCH - 1 - s
        blk = c // 4
        col = slice(c, c + 1)
        kT = KaT[:, c * 128:(c + 1) * 128]
        qT = QaT[:, c * 128:(c + 1) * 128]
        gb, Dm, E, Li = R.T
        In, InT, Kdec, vn = R.bf
        p01, p2, p3 = R.p01, R.p2, R.p3
        S.op("act", lambda e: e.activation(out=gb[:], in_=cf(C_ONES), func=AF.Copy, scale=g[d][:, col]), [cst, g[d]], [gb])
        yield
        mm(p2, gb[:], cf(C_TRIU if d == 0 else C_TRIL), [gb, cst])
        yield
        S.op("dve", lambda e: e.scalar_tensor_tensor(out=Dm[:], in0=p2[:], scalar=G[d][:, col], in1=cf(C_MF if d == 0 else C_MB),
                                                     op0=ALU.subtract, op1=ALU.add), [p2, G[d], cst], [Dm])
        yield
        S.op("act", lambda e: e.activation(out=E[:], in_=Dm[:], func=AF.Exp, scale=-1.0), [Dm], [E])
        yield
        S.op("pe", lambda e: e.matmul(p01[:, 0:128], lhsT=kT, rhs=kT, start=True, stop=True), [KaTb[blk]], [p01])
        yield
        S.op("pe", lambda e: e.matmul(p01[:, 128:256], lhsT=qT, rhs=kT, start=True, stop=True), [QaTb[blk], KaTb[blk]], [p01])
        yield
        S.op("dve", lambda e: e.scalar_tensor_tensor(out=Li[:], in0=p01[:, 0:128], scalar=beta[d][:, col], in1=E[:],
                                                     op0=ALU.mult, op1=ALU.mult), [p01, beta[d], E], [Li])
        yield
        S.op("dve", lambda e: e.tensor_tensor(out=In[:], in0=p01[:, 128:256], in1=E[:], op=ALU.mult), [p01, E], [In])
        yield
        PA = R.PA
        S.op("pool", lambda e: e.tensor_tensor(out=PA[:, 0:128], in0=Li[:], in1=cf(C_OFFD), op=ALU.mult), [Li, cst], [PA])
        yield
        S.op("pe", lambda e: e.transpose(p2[:], PA[:, 0:128], cf(C_ID)), [PA, cst], [p2])
        yield
        p3v = p3.ap.bitcast(BF16)[:, 0:128]
        tr(p3v, p3, In[:], [In])
        yield
        S.op("act", lambda e: e.copy(out=PA[:, 128:256], in_=p2[:]), [p2], [PA])
        yield
        RT = R.RA
        S.op("dve", lambda e: e.tensor_tensor(out=RT[:], in0=cf(C_ID), in1=p2[:], op=ALU.subtract), [cst, p2], [RT])
        yield
        S.op("act", lambda e: e.copy(out=InT[:], in_=p3v), [p3], [InT])
        yield
        src = PA
        for k in range(1, 7):
            dstp = R.PB if k % 2 == 1 else R.PC
            S.op("pe", lambda e: e.matmul(p01[:, 0:128], lhsT=src[:, 128:256], rhs=src[:, 0:128], start=True, stop=True), [src], [p01])
            yield
            w = 128
            if k < 6:
                S.op("pe", lambda e: e.matmul(p01[:, 128:256], lhsT=src[:, 0:128], rhs=src[:, 128:256], start=True, stop=True), [src], [p01])
                yield
                w = 256
            S.op("act", lambda e: e.copy(out=dstp[:, 0:w], in_=p01[:, 0:w]), [p01], [dstp])
            yield
            mm(p2, dstp[:, 0:128], RT[:], [dstp, RT])
            yield
            RTn = R.RB if RT is R.RA else R.RA
            S.op("dve", lambda e: e.tensor_tensor(out=RTn[:], in0=p2[:], in1=RT[:], op=ALU.add), [p2, RT], [RTn])
            yield
            src, RT = dstp, RTn
        TmT = RT
        Vb32, Kbg32 = gb, Dm
        S.op("act", lambda e: e.activation(out=Vb32[:], in_=Vtok[:, c, :], func=AF.Copy, scale=beta[d][:, col]),
             [Vtokb[c], beta[d]], [Vb32])
        yield
        S.op("pool", lambda e: e.tensor_scalar(out=Kbg32[:], in0=Ktok[:, c, :], scalar1=bg[d][:, col], scalar2=None, op0=ALU.mult),
             [Ktokb[c], bg[d]], [Kbg32])
        yield
        S.op("dve", lambda e: e.tensor_scalar(out=Kdec[:], in0=Ktok[:, c, :], scalar1=kd[d][:, col], scalar2=None, op0=ALU.mult),
             [Ktokb[c], kd[d]], [Kdec])
        yield
        S.op("pe", lambda e: e.matmul(p01[:, 0:128], lhsT=TmT[:], rhs=Vb32[:], start=True, stop=True), [TmT, Vb32], [p01])
        yield
        S.op("pe", lambda e: e.matmul(p01[:, 128:256], lhsT=Kbg32[:], rhs=TmT[:], start=True, stop=True), [Kbg32, TmT], [p01])
        yield
        UW = R.UW
        S.op("act", lambda e: e.copy(out=UW[:], in_=p01[:]), [p01], [UW])
        yield
        while chain_done[d] != s - 1:
            yield
        pWS, pQS, pIV, pKV = R.pq
        mm(pWS, UW[:, 128:256], Sst[d][:], [UW, Sst[d]])
        yield
        mm(pQS, qT, Sbf[d][:], [QaTb[blk], Sbf[d]])
        yield
        S.op("dve", lambda e: e.tensor_tensor(out=vn[:], in0=UW[:, 0:128], in1=pWS[:], op=ALU.subtract), [UW, pWS], [vn])
        yield
        mm(pIV, InT[:], vn[:], [InT, vn])
        yield
        mm(pKV, Kdec[:], vn[:], [Kdec, vn])
        yield
        S.op("dve", lambda e: e.scalar_tensor_tensor(out=Sst[d][:], in0=Sst[d][:], scalar=eGl[d][:, col], in1=pKV[:],
                                                     op0=ALU.mult, op1=ALU.add), [Sst[d], eGl[d], pKV], [Sst[d]])
        yield
        S.op("act", lambda e: e.copy(out=Sbf[d][:], in_=Sst[d][:]), [Sst[d]], [Sbf[d]])
        yield
        S.op("dve", lambda e: e.scalar_tensor_tensor(out=oacc[:, c, :], in0=pQS[:], scalar=eG[d][:, col], in1=oacc[:, c, :],
                                                     op0=ALU.mult, op1=ALU.add), [pQS, eG[d], oaccb[c]], [oaccb[c]])
        yield
        S.op("dve", lambda e: e.tensor_tensor(out=oacc[:, c, :], in0=pIV[:], in1=oacc[:, c, :], op=ALU.add),
             [pIV, oaccb[c]], [oaccb[c]])
        yield
        chain_done[d] = s

    JOBLEN = 72
    NST = NSLOT // 2
    active = []
    nxt = 0
    rounds = 0
    last_launch = -10 ** 9
    while nxt < nsteps or active:
        if nxt < nsteps and len(active) <= NSLOT - 2 and (not active or rounds - last_launch >= JOBLEN // NST):
            for d_ in range(2):
                k_ = 2 * (nxt % NST) + d_
                assert all(k_ != kk for kk, _ in active), "slot still busy"
                active.append((k_, job(nxt, d_, slots[k_])))
            nxt += 1
            last_launch = rounds
        rounds += 1
        still = []
        for k_, gjob in active:
            try:
                next(gjob)
                still.append((k_, gjob))
            except StopIteration:
                pass
        active = still
    r_bf = Ring(slots[0].bf)
    psr = Ring([Buf(banks[6 + (i % 2)][:, (i // 2) * 128:(i // 2 + 1) * 128], excl=banks[6 + (i % 2)]) for i in range(8)])

    S.barrier()
    oT = VaT
    oTb = [Buf(oT[:, b * 512:(b + 1) * 512]) for b in range(NB)]
    junk = Buf(A.sb([128, 128], F32, "junkg"))
    ssall = Buf(A.sb([128, NCH], F32, "ssall"))
    szb_all = Buf(sz)
    S.op("pool", lambda e: e.tensor_tensor(out=sz[:], in0=sz[:], in1=gw[:].unsqueeze(1).broadcast_to([128, NCH, 128]), op=ALU.mult),
         szb + [gw], szb)
    for c in range(NCH):
        S.op("act", lambda e: e.activation(out=junk[:], in_=oacc[:, c, :], func=AF.Square, accum_out=ssall[:, c:c + 1]),
             [oaccb[c]], [junk, ssall])
    S.op("act", lambda e: e.activation(out=ssall[:], in_=ssall[:], func=AF.Sqrt, scale=1.0 / 128, bias=EPS), [ssall], [ssall])
    S.op("dve", lambda e: e.reciprocal(out=ssall[:], in_=ssall[:]), [ssall], [ssall])
    trO = [Buf(banks[6 + i].ap.bitcast(BF16)[:, 0:512].rearrange("p (t d) -> p t d", t=4), excl=banks[6 + i]) for i in range(2)]
    for c in range(NCH):
        ob = r_bf.get()
        S.op("dve", lambda e: e.scalar_tensor_tensor(out=ob[:], in0=oacc[:, c, :], scalar=ssall[:, c:c + 1], in1=sz[:, c, :],
                                                     op0=ALU.mult, op1=ALU.mult), [oaccb[c], ssall, szb[c]], [ob])
        tp = trO[(c // 4) % 2]
        S.op("pe", lambda e: e.transpose(tp[:, c % 4, :], ob[:], cb(C_ID)), [ob, cstb], [tp], acc=(c % 4 != 0))
        if c % 4 == 3:
            b_ = c // 4
            S.op("act", lambda e: e.copy(out=oT[:, b_ * 512:(b_ + 1) * 512].rearrange("p (t d) -> p t d", t=4), in_=tp[:]),
                 [tp], [oTb[b_]])
    io["store_oa"](oT, oTb)


def _ext_mixer_io(nc, S, out_name):
    hT_d = Buf(nc.dram_tensor("hT", [D, SEQ], BF16, kind="ExternalInput").ap())
    oT_d = Buf(nc.dram_tensor(out_name, [128, SEQ], BF16, kind="ExternalOutput").ap())
    hv = hT_d.ap.rearrange("(c p) t -> p c t", p=128)

    def load_h(b, hb):
        S.dma("sp", hb, hb[:], hT_d, hv[:, :, b * 512:(b + 1) * 512])

    def store(oT, oTb, quarter=None):
        for b in (range(NB) if quarter is None else range(4 * quarter, 4 * quarter + 4)):
            S.dma("sp", oT_d, oT_d[:, b * 512:(b + 1) * 512], oTb[b], oT[:, b * 512:(b + 1) * 512])

    return hT_d, oT_d, load_h, store


def build_G(nsteps=NCH, plimit=9):
    nc = bass.Bass("TRN2", target_bir_lowering=False)
    S = Sched(nc)
    A = Alloc(nc)
    banks = make_banks(A)
    hT_d, oT_d, load_h, store = _ext_mixer_io(nc, S, "oaT")
    io = {
        "gw_d": Buf(nc.dram_tensor("w", [D, 516], F32, kind="ExternalInput").ap()),
        "cw_d": Buf(nc.dram_tensor("cw", [128, 15], F32, kind="ExternalInput").ap()),
        "par_d": Buf(nc.dram_tensor("par", [128, 4], F32, kind="ExternalInput").ap()),
        "gnw_d": Buf(nc.dram_tensor("gw", [128, 128], F32, kind="ExternalInput").ap()),
        "cst_d": Buf(nc.dram_tensor("cst", [128, 8, 128], F32, kind="ExternalInput").ap()),
        "load_h": load_h, "store_oa": store,
    }
    emit_G(nc, S, A, banks, io, nsteps, plimit)
    S.wait_dma("sp", [oT_d])
    return nc


def g_inputs(inp, j, hT_full, cst):
    w_in = inp["w_in"][0]
    cols = np.concatenate([np.arange(j * 128, (j + 1) * 128), 512 + np.arange(j * 128, (j + 1) * 128),
                           1024 + np.arange(j * 128, (j + 1) * 128), 1536 + np.arange(j * 128, (j + 1) * 128),
                           np.array([2048 + j, 2052 + j, 2056 + j, 2060 + j])])
    w = np.ascontiguousarray(w_in[:, cols])
    cwf = inp["conv_w"][0]
    cw = np.stack([cwf[m * 512 + j * 128:m * 512 + (j + 1) * 128, :] for m in range(3)], axis=1).reshape(128, 15)
    par = np.array([inp["a_log"][0, 0, j], inp["a_log"][0, 1, j], inp["dt_bias"][0, 0, j], inp["dt_bias"][0, 1, j]], np.float32)
    par = np.ascontiguousarray(np.broadcast_to(par[None, :], (128, 4)))
    gw = np.ascontiguousarray(np.broadcast_to(inp["gdn_norm_w"][0][None, :], (128, 128)))
    return {"hT": hT_full, "w": w, "cw": np.ascontiguousarray(cw), "par": par, "gw": gw, "cst": cst}


DILS = (1, 4, 16)
KPAD = 1024
NEG = -30000.0


def t5_bucket_np(rel):
    nb = 16
    bucket = (rel > 0).astype(np.int32) * nb
    n = np.abs(rel)
    max_exact = nb // 2
    large = max_exact + (np.log(np.maximum(n, 1) / max_exact) / np.log(1024 / max_exact) * (nb - max_exact)).astype(np.int32)
    large = np.minimum(large, nb - 1)
    return (bucket + np.where(n < max_exact, n, large)).astype(np.int32)


def swa_tables(rel_bias, j):
    kk = np.arange(128)[:, None]
    qq = np.arange(128)[None, :]
    bias_g = np.zeros((128, 3 * 2 * 2, 128), np.float32)
    maskc = np.zeros((128, 4, 128), np.float32)
    for pos in range(2):
        rel = kk - 64 - qq if pos == 0 else kk + 64 - qq
        valid = np.abs(rel) <= 64
        for var in range(2):
            v = valid.copy()
            if var == 1:
                if pos == 0:
                    v &= (kk >= 64)
                else:
                    v &= (kk < 64)
            maskc[:, pos * 2 + var, :] = np.where(v, 0.0, NEG)
        for di, dil in enumerate(DILS):
            bk = t5_bucket_np(rel * dil)
            for hd in range(2):
                bias_g[:, (di * 2 + pos) * 2 + hd, :] = rel_bias[bk, 2 * j + hd]
    return bias_g, maskc


def emit_W(nc, S, A, banks, io, npat=3):
    w_d, nw_d, bg_d, mk_d, cst_d = io["ww_d"], io["nw_d"], io["bg_d"], io["mk_d"], io["cst_d"]
    Vd, Rd = io["Vd"], io["Rd"]

    cst = Buf(A.sb([128, 8, 128], F32, "cst"))
    cstb = Buf(A.sb([128, 8, 128], BF16, "cstb"))
    W = Buf(A.sb([128, KC, 384], BF16, "W"))
    nw = Buf(A.sb([128, 2], F32, "nw"))
    tabc = Buf(A.sb([128, 18, 256], F32, "tabc"))
    QbT = A.sb([128, SEQ], BF16, "QbT")
    KbT = A.sb([128, SEQ + 2 * KPAD], BF16, "KbT")
    QbTb = Buf(QbT)
    KbTb = Buf(KbT)
    RW1 = A.sb([128, 2 * KC * 512], BF16, "RW1")
    hblk = [Buf(RW1[:, i * KC * 512:(i + 1) * KC * 512].rearrange("p (c t) -> p c t", c=KC)) for i in range(2)]
    Vst = [Buf(A.sb([128, 4, 130], BF16, "Vst")) for _ in range(2)]
    Te = Buf(A.sb([128, NCH + 1, 130], BF16, "Te"))
    res_ap = A.sb([128, NCH * 130], F32, "res")
    res = Buf(res_ap.rearrange("p (n c) -> p n c", c=130))
    bgt = Buf(res_ap[:, 0:1536].rearrange("p (k c) -> p k c", c=128))
    mkt = Buf(res_ap[:, 1536:2048].rearrange("p (k c) -> p k c", c=128))
    acc = Buf(A.sb([128, NCH, 130], F32, "acc"))
    RW2 = A.sb([128, 10240], BF16, "RW2")
    RW2f = RW2.bitcast(F32)
    tmp512 = Ring([Buf(RW2f[:, i * 512:(i + 1) * 512]) for i in range(4)])
    qfr = Ring([Buf(RW2f[:, (4 + i) * 512:(5 + i) * 512]) for i in range(6)])
    Te2 = Buf(RW2[:, 0:(NCH + 1) * 130].rearrange("p (k c) -> p k c", c=130))
    pbr = Ring(banks[0:4])

    def cf(k):
        return cst[:, k, :]

    def cb(k):
        return cstb[:, k, :]

    S.dma("sp", cst, cst[:], cst_d, cst_d[:, :, :])
    S.dma("pool", cstb, cstb[:], cst_d, cst_d[:, :, :])
    S.dma("pool", W, W[:], w_d, w_d.ap.rearrange("(c p) n -> p c n", p=128))
    S.dma("sp", nw, nw[:], nw_d, nw_d[:, :])
    S.dma("sp", bgt, bgt[:], bg_d, bg_d[:, :, :])
    S.dma("sp", mkt, mkt[:], mk_d, mk_d[:, :, :])
    for di in range(3):
        for hd in range(2):
            for v3, (vlo, vhi) in enumerate(((0, 0), (1, 0), (0, 1))):
                ti = (di * 2 + hd) * 3 + v3
                for pos, var in ((0, vlo), (1, vhi)):
                    S.op("pool", lambda e: e.tensor_tensor(out=tabc[:, ti, pos * 128:(pos + 1) * 128],
                                                           in0=bgt[:, (di * 2 + pos) * 2 + hd, :],
                                                           in1=mkt[:, pos * 2 + var, :], op=ALU.add), [bgt, mkt], [tabc])
    S.op("pool", lambda e: e.memset(KbT[:, 0:KPAD], 0.0), [], [KbTb])
    S.op("pool", lambda e: e.memset(KbT[:, KPAD + SEQ:], 0.0), [], [KbTb])
    for i in range(2):
        S.op("pool", lambda e: e.memset(Vst[i][:], 1.0), [], [Vst[i]])
    S.op("pool", lambda e: e.memset(Te[:], 0.0), [], [Te])
    if "after_setup" in io:
        io["after_setup"]()

    Vdv = Vd.ap.rearrange("(t p) c -> p t c", p=128)
    st = {}

    def main(b):
        hb = hblk[b % 2]
        io["load_h"](b, hb)
        bq, bk, bv = banks[4 * (b % 2) + 0], banks[4 * (b % 2) + 1], banks[4 * (b % 2) + 2]
        for m, p in ((0, bq), (1, bk)):
            for c in range(KC):
                S.op("pe", lambda e: e.matmul(p[:], lhsT=W[:, c, m * 128:(m + 1) * 128], rhs=hb[:, c, :],
                                              start=(c == 0), stop=(c == KC - 1)), [W, hb], [p], acc=(c > 0))
        for t in range(4):
            for c in range(KC):
                S.op("pe", lambda e: e.matmul(bv[:, t * 128:(t + 1) * 128], lhsT=hb[:, c, t * 128:(t + 1) * 128], rhs=W[:, c, 256:384],
                                              start=(c == 0), stop=(c == KC - 1)), [W, hb], [bv], acc=(c > 0 or t > 0))
        sqs = []
        for m, p in ((0, bq), (1, bk)):
            sq = tmp512.get()
            qf = qfr.get()
            S.op("act", lambda e: e.activation(out=sq[:], in_=p[:], func=AF.Square), [p], [sq])
            S.op("act", lambda e: e.copy(out=qf[:], in_=p[:]), [p], [qf])
            sqs.append((sq, qf))
        vs = Vst[b % 2]
        S.op("act", lambda e: e.copy(out=vs[:].rearrange("p t (h c) -> p t h c", h=2)[:, :, :, 0:64],
                                     in_=bv[:].rearrange("p (t h c) -> p t h c", t=4, h=2)), [bv], [vs])
        S.dma("act", Vd, Vdv[:, b * 4:(b + 1) * 4, :], vs, vs[:])
        st[b] = sqs

    def chain(b):
        bs = banks[4 * (b % 2) + 3]
        sqs = st.pop(b)
        for m in range(2):
            sq, p = sqs[m]
            S.op("pe", lambda e: e.matmul(bs[:], lhsT=cf(C_BLK), rhs=sq[:], start=True, stop=True), [cst, sq], [bs])
            if m == 0:
                S.op("act", lambda e: e.activation(out=sq[:], in_=bs[:], func=AF.Sqrt, scale=1.0, bias=64.0 * EPS), [bs], [sq])
            else:
                S.op("act", lambda e: e.activation(out=sq[:], in_=bs[:], func=AF.Sqrt, scale=1.0 / 64, bias=EPS), [bs], [sq])
            S.op("dve", lambda e: e.reciprocal(out=sq[:], in_=sq[:]), [sq], [sq])
            dst = QbT[:, b * 512:(b + 1) * 512] if m == 0 else KbT[:, KPAD + b * 512:KPAD + (b + 1) * 512]
            S.op("dve", lambda e: e.scalar_tensor_tensor(out=dst, in0=p[:], scalar=nw[:, m:m + 1], in1=sq[:],
                                                         op0=ALU.mult, op1=ALU.mult), [p, nw, sq], [QbTb if m == 0 else KbTb])

    for b in range(NB + 1):
        if b < NB:
            main(b)
        if b >= 1:
            chain(b - 1)

    S.barrier()
    psr = Ring([Buf(banks[i % 4][:, (i // 4) * 256:(i // 4 + 1) * 256], excl=banks[i % 4]) for i in range(8)])
    psr1 = Ring([Buf(banks[i % 4][:, (i // 4) * 128:(i // 4 + 1) * 128], excl=banks[i % 4]) for i in range(16)])
    por = Ring([banks[4], banks[5], banks[6], banks[7]])
    sbr = Ring([Buf(A.sb([128, 256], F32, "sb")) for _ in range(4)])
    ppr = Ring([Buf(A.sb([128, 256], BF16, "pp")) for _ in range(8)])
    S.op("pool", lambda e: e.memset(Te2[:], 0.0), [], [Te2])
    Rd2 = io["Rd2"]
    TeB = [Te, Te2]

    def load_T(di, Tb):
        dil = DILS[di]
        nbs = (SEQ // dil) // 128
        for r in range(dil):
            seg = Vd.ap.rearrange("(t d) c -> d t c", d=dil)[r]
            segv = seg.rearrange("(k i) c -> i k c", i=128)
            k0 = r * nbs
            step = min(nbs, 8)
            for ks in range(0, nbs, step):
                S.dma("sp", Tb, Tb[64:128, k0 + ks:k0 + ks + step, :], Vd, segv[0:64, ks:ks + step, :])
                S.dma("sp", Tb, Tb[0:64, k0 + 1 + ks:k0 + 1 + ks + step, :], Vd, segv[64:128, ks:ks + step, :])

    order = [1, 2, 0]
    load_T(order[0], TeB[0])
    for idx, di in enumerate(order):
        dil = DILS[di]
        L = SEQ // dil
        nbs = L // 128
        Te = TeB[idx % 2]
        if idx + 1 < 3:
            load_T(order[idx + 1], TeB[(idx + 1) % 2])
        dst = (res, acc, res)[idx]
        Rdx = (Rd, Rd2, None)[idx]
        items = [(n, hd) for n in range(NCH) for hd in range(2)]
        LAG = 4
        pend = {}

        def stage1(n, hd):
            r = (128 * n) // L
            t0 = 128 * n - r * L
            hs = slice(hd * 64, (hd + 1) * 64)
            q0 = r + dil * t0
            qT = QbT[hs, q0:q0 + dil * 127 + 1:dil]
            v3 = 1 if t0 == 0 else (2 if t0 + 128 == L else 0)
            ti = (di * 2 + hd) * 3 + v3
            ps_ = psr.get()
            for pos in range(2):
                kk0 = KPAD + r + dil * (t0 - 64 + 128 * pos)
                kT = KbT[hs, kk0:kk0 + dil * 127 + 1:dil]
                S.op("pe", lambda e: e.matmul(ps_[:, pos * 128:(pos + 1) * 128], lhsT=kT, rhs=qT, start=True, stop=True),
                     [KbTb, QbTb], [ps_])
            sb_ = sbr.get()
            S.op("dve", lambda e: e.tensor_tensor(out=sb_[:], in0=ps_[:], in1=tabc[:, ti, :], op=ALU.add), [ps_, tabc], [sb_])
            pp_ = ppr.get()
            S.op("act", lambda e: e.activation(out=pp_[:], in_=sb_[:], func=AF.Exp), [sb_], [pp_])
            pend[(n, hd)] = pp_

        def stage2(n, hd):
            pp_ = pend.pop((n, hd))
            po = por.get()
            for pos in range(2):
                S.op("pe", lambda e: e.matmul(po[:, 0:65], lhsT=pp_[:, pos * 128:(pos + 1) * 128], rhs=Te[:, n + pos, hd * 65:(hd + 1) * 65],
                                              start=(pos == 0), stop=(pos == 1)), [pp_, Te], [po], acc=(pos == 1))
            if n % 2 == 0:
                S.op("act", lambda e: e.copy(out=dst[:, n, hd * 65:(hd + 1) * 65], in_=po[:, 0:65]), [po], [dst])
            else:
                S.op("dve", lambda e: e.tensor_copy(out=dst[:, n, hd * 65:(hd + 1) * 65], in_=po[:, 0:65]), [po], [dst])

        for k in range(len(items) + LAG):
            if k < len(items):
                stage1(*items[k])
            if k >= LAG:
                stage2(*items[k - LAG])
        if Rdx is not None:
            for r in range(dil):
                seg = Rdx.ap.rearrange("(t d) c -> d t c", d=dil)[r]
                segv = seg.rearrange("(k i) c -> i k c", i=128)
                k0 = r * nbs
                S.dma("sp", Rdx, segv[:, :, :], dst, dst[:, k0:k0 + nbs, :])
    for Rdx in (Rd2, Rd):
        Rdv = Rdx.ap.rearrange("(n p) c -> p n c", p=128)
        for q4 in range(4):
            S.dma("sp", acc, acc[:, q4 * 16:(q4 + 1) * 16, :], Rdx, Rdv[:, q4 * 16:(q4 + 1) * 16, :])
        S.op("dve", lambda e: e.tensor_tensor(out=res[:], in0=res[:], in1=acc[:], op=ALU.add), [acc, res], [res])
    acc = res

    rden = Buf(A.sb([128, NCH, 2], F32, "rden"))
    S.op("dve", lambda e: e.reciprocal(out=rden[:], in_=acc[:].rearrange("p n (h c) -> p n h c", h=2)[:, :, :, 64]), [acc], [rden])
    oT = RW1
    oTb = [Buf(oT[:, b * 512:(b + 1) * 512]) for b in range(NB)]
    accv = acc[:].rearrange("p n (h c) -> p n h c", h=2)
    trO = [Buf(banks[i].ap.bitcast(BF16)[:, 0:512].rearrange("p (t d) -> p t d", t=4), excl=banks[i]) for i in range(2)]
    for n in range(NCH):
        ob = ppr.get()
        S.op("dve", lambda e: e.tensor_tensor(out=ob[:, 0:128].rearrange("p (h c) -> p h c", h=2), in0=accv[:, n, :, 0:64],
                                              in1=rden[:, n, :].unsqueeze(2).broadcast_to([128, 2, 64]), op=ALU.mult),
             [acc, rden], [ob])
        tp = trO[(n // 4) % 2]
        S.op("pe", lambda e: e.transpose(tp[:, n % 4, :], ob[:, 0:128], cb(C_ID)), [ob, cstb], [tp], acc=(n % 4 != 0))
        if n % 4 == 3:
            b_ = n // 4
            S.op("act", lambda e: e.copy(out=oT[:, b_ * 512:(b_ + 1) * 512].rearrange("p (t d) -> p t d", t=4), in_=tp[:]),
                 [tp], [oTb[b_]])
        if n % 16 == 15:
            io["store_ob"](oT, oTb, n // 16)


def build_W(npat=3):
    nc = bass.Bass("TRN2", target_bir_lowering=False)
    S = Sched(nc)
    A = Alloc(nc)
    banks = make_banks(A)
    hT_d, oT_d, load_h, store = _ext_mixer_io(nc, S, "obT")
    io = {
        "ww_d": Buf(nc.dram_tensor("w", [D, 384], F32, kind="ExternalInput").ap()),
        "nw_d": Buf(nc.dram_tensor("nw", [128, 2], F32, kind="ExternalInput").ap()),
        "bg_d": Buf(nc.dram_tensor("bias_g", [128, 12, 128], F32, kind="ExternalInput").ap()),
        "mk_d": Buf(nc.dram_tensor("maskc", [128, 4, 128], F32, kind="ExternalInput").ap()),
        "cst_d": Buf(nc.dram_tensor("cst", [128, 8, 128], F32, kind="ExternalInput").ap()),
        "Vd": Buf(nc.dram_tensor("Vd", [SEQ, 130], BF16).ap()),
        "Rd": Buf(nc.dram_tensor("Rd", [SEQ, 130], F32).ap()),
        "Rd2": Buf(nc.dram_tensor("Rd2", [SEQ, 130], F32).ap()),
        "load_h": load_h, "store_ob": store,
    }
    emit_W(nc, S, A, banks, io, npat)
    S.wait_dma("sp", [oT_d])
    return nc


def w_inputs(inp, j, hT_full, cst):
    w_in = inp["w_in"][0]
    base = 2064
    cols = np.concatenate([base + np.arange(j * 128, (j + 1) * 128), base + 512 + np.arange(j * 128, (j + 1) * 128),
                           base + 1024 + np.arange(j * 128, (j + 1) * 128)])
    w = np.ascontiguousarray(w_in[:, cols])
    nw = np.stack([np.tile(inp["q_norm_w"][0], 2), np.tile(inp["k_norm_w"][0], 2)], axis=1).astype(np.float32)
    bias_g, maskc = swa_tables(inp["rel_bias"], j)
    return {"hT": hT_full, "w": w, "nw": np.ascontiguousarray(nw), "bias_g": bias_g, "maskc": maskc, "cst": cst}


GROUPS = [[0, 1, 2, 3], [4, 5, 6, 7]]


def build_fused():
    nc = bass.Bass("TRN2", target_bir_lowering=False)
    S = Sched(nc)
    A = Alloc(nc)
    banks = make_banks(A)

    def ext(name, shape, dt=F32):
        return Buf(nc.dram_tensor(name, list(shape), dt, kind="ExternalInput").ap(), name)

    e = {
        "x": ext("x", [TOK, D]), "n1": ext("n1", [128, D]), "nm": ext("nm", [128, D]),
        "wg1": ext("wg1", [D, FF]), "wu1": ext("wu1", [D, FF]), "wd1": ext("wd1", [FF, D]),
        "ident": ext("ident", [128, 128]), "cst": ext("cst", [128, 8, 128]),
        "gw": ext("gw", [D, 516]), "cw": ext("cw", [128, 15]), "par": ext("par", [128, 4]), "gnw": ext("gnw", [128, 128]),
        "ww": ext("ww", [D, 384]), "nw": ext("nw", [128, 2]), "bias_g": ext("bias_g", [128, 12, 128]),
        "maskc": ext("maskc", [128, 4, 128]),
        "wo": ext("wo", [D, D]), "n2": ext("n2", [128, D]), "nf": ext("nf", [128, D]),
        "wg2": ext("wg2", [D, FF]), "wu2": ext("wu2", [D, FF]), "wd2": ext("wd2", [FF, D]),
    }
    out_d = Buf(nc.dram_tensor("out", [TOK, D], F32, kind="ExternalOutput").ap(), "out")
    out_d.persist = True
    x1_d = Buf(nc.dram_tensor("x1s", [TOK, D], F32).ap(), "x1s")
    x1_d.persist = True
    hb_t = [nc.dram_tensor(f"hbnc{i}", [1024, 256], F32) for i in range(4)]
    hbB = [Buf(t.ap(), f"hbnc{i}") for i, t in enumerate(hb_t)]
    HG = nc.dram_tensor("hgath", [4, 4096, 256], F32)
    hgB = [Buf(HG.ap()[i], f"hg{i}") for i in range(4)]
    oa_t = [nc.dram_tensor(f"oanc{i}", [128, 1024], F32) for i in range(4)]
    oaB = [Buf(t.ap(), f"oanc{i}") for i, t in enumerate(oa_t)]
    ob_t = [nc.dram_tensor(f"obnc{i}", [128, 1024], F32) for i in range(4)]
    obB = [Buf(t.ap(), f"obnc{i}") for i, t in enumerate(ob_t)]
    OGa = nc.dram_tensor("ogath_a", [4, 512, 1024], F32)
    OGb = nc.dram_tensor("ogath_b", [4, 512, 1024], F32)
    ogaB = Buf(OGa.ap(), "oga")
    ogbB = Buf(OGb.ap(), "ogb")
    for b in hbB + oaB + obB:
        b.persist = True
    cc_sem = nc.alloc_semaphore("cc_sem")
    cc = {"n": 0}

    def gather(src_t, srcB, dst_ap, dstB):
        S.wait_dma("pool", [srcB])
        ins = nc.gpsimd.collective_compute("AllGather", ALU.bypass, replica_groups=GROUPS,
                                           ins=[src_t.ap().opt()], outs=[dst_ap.opt()])
        ins.then_inc(cc_sem)
        cc["n"] += 1
        dstB.dsem = cc_sem
        dstB.dcnt = cc["n"]

    A.begin()

    def store_hT(C, tb):
        dv = hb_t[tb].ap().bitcast(BF16).rearrange("(c p) t -> p c t", p=128)
        S.dma("sp", hbB[tb], dv, C.nTb[tb], C.nT[:, :, tb * 512:(tb + 1) * 512])
        gather(hb_t[tb], hbB[tb], HG.ap()[tb], hgB[tb])

    emit_A(nc, S, A, banks, {"x_d": e["x"], "n1_d": e["n1"], "nm_d": e["nm"], "wg_d": e["wg1"], "wu_d": e["wu1"],
                             "wd_d": e["wd1"], "id_d": e["ident"], "x1_d": x1_d, "store_hT": store_hT})
    S.end_phase()
    A.end()

    def load_h(b, hb):
        r, bb = b // 4, b % 4
        src = HG.ap()[bb].bitcast(BF16).rearrange("(r c p) t -> r p c t", r=4, c=KC)[r]
        S.dma("sp", hb, hb[:], hgB[bb], src)

    def make_store(bnc_t, bncB, OGx, ogxB, do_gather=True):
        def store(oT, oTb, quarter=None):
            for q in (range(4) if quarter is None else [quarter]):
                dv = bnc_t[q].ap().bitcast(BF16)
                for bb in range(4):
                    b = 4 * q + bb
                    S.dma("sp", bncB[q], dv[:, bb * 512:(bb + 1) * 512], oTb[b], oT[:, b * 512:(b + 1) * 512])
                if do_gather:
                    gather(bnc_t[q], bncB[q], OGx.ap()[q], ogxB)
        return store

    def oa_gathers():
        for q in range(4):
            gather(oa_t[q], oaB[q], OGa.ap()[q], ogaB)

    A.begin()
    emit_G(nc, S, A, banks, {"gw_d": e["gw"], "cw_d": e["cw"], "par_d": e["par"], "gnw_d": e["gnw"], "cst_d": e["cst"],
                             "load_h": load_h, "store_oa": make_store(oa_t, oaB, OGa, ogaB, do_gather=False)})
    S.end_phase()
    A.end()
    A.begin()
    Vd = Buf(nc.dram_tensor("Vd", [SEQ, 130], BF16).ap(), "Vd")
    Rd = Buf(nc.dram_tensor("Rd", [SEQ, 130], F32).ap(), "Rd")
    Rd2 = Buf(nc.dram_tensor("Rd2", [SEQ, 130], F32).ap(), "Rd2")
    emit_W(nc, S, A, banks, {"ww_d": e["ww"], "nw_d": e["nw"], "bg_d": e["bias_g"], "mk_d": e["maskc"], "cst_d": e["cst"],
                             "Vd": Vd, "Rd": Rd, "Rd2": Rd2, "after_setup": oa_gathers, "load_h": load_h, "store_ob": make_store(ob_t, obB, OGb, ogbB)})
    S.end_phase()
    A.end()

    A.begin()

    def load_oT(C):
        qv = nc.sync.partition_id() % 4
        for sidx, (OGx, ogxB) in enumerate(((OGa, ogaB), (OGb, ogbB))):
            src = OGx.ap().bitcast(BF16)[qv].rearrange("(r p) t -> p r t", p=128)
            for tb in range(4):
                S.dma("sp", C.nTb[tb], C.nT[:, sidx::2, tb * 512:(tb + 1) * 512], ogxB, src[:, :, tb * 512:(tb + 1) * 512])

    emit_C(nc, S, A, banks, {"x_d": x1_d, "wo_d": e["wo"], "n2_d": e["n2"], "nf_d": e["nf"], "wg_d": e["wg2"],
                             "wu_d": e["wu2"], "wd_d": e["wd2"], "id_d": e["ident"], "out_d": out_d,
                             "wo_rows": lambda c: (c // 2) * 128 + (c % 2) * 512, "load_oT": load_oT})
    S.wait_dma("sp", [out_d])
    A.end()
    return nc


def _rep(v):
    return np.ascontiguousarray(np.broadcast_to(np.asarray(v, np.float32).reshape(1, -1), (128, v.size)))


def kernel(**inp):
    inp = {k: np.asarray(v) for k, v in inp.items()}
    cores = list(range(8))
    x = np.ascontiguousarray(inp["x"], dtype=np.float32).reshape(8, TOK, D)
    ident = np.eye(128, dtype=np.float32)
    cst = host_consts()
    shared = {
        "n1": _rep(inp["ffn1_norm"][0]), "nm": _rep(inp["mix_norm"][0]),
        "wg1": inp["ffn1_w_gate"][0], "wu1": inp["ffn1_w_up"][0], "wd1": inp["ffn1_w_down"][0],
        "ident": ident, "cst": cst,
        "wo": inp["w_out"][0], "n2": _rep(inp["ffn2_norm"][0]), "nf": _rep(inp["final_norm"][0]),
        "wg2": inp["ffn2_w_gate"][0], "wu2": inp["ffn2_w_up"][0], "wd2": inp["ffn2_w_down"][0],
    }
    maps = []
    for c in cores:
        j = c % 4
        g = g_inputs(inp, j, None, cst)
        w = w_inputs(inp, j, None, cst)
        m = dict(shared)
        m.update({"x": x[c], "gw": g["w"], "cw": g["cw"], "par": g["par"], "gnw": g["gw"],
                  "ww": w["w"], "nw": w["nw"], "bias_g": w["bias_g"], "maskc": w["maskc"]})
        maps.append({k: np.ascontiguousarray(v, dtype=np.float32) for k, v in m.items()})
    nc = build_fused()
    res = run_bass_kernel_spmd(nc, maps, core_ids=cores).results
    return np.stack([res[c]["out"] for c in cores]).reshape(2, SEQ, D).astype(np.float32)
```
